# Optimizing a Trainium2 kernel written in Bass

```python
import jax, jax.numpy as jnp
from jax import lax
import numpy as np

D_MODEL = 1024
BATCH = 4
SEQ = 4096
DEPTH = 2
DEC_BATCH = 32
DEC_SEQ = 64
PAST_LEN = 1024

CHUNK = 64
N_META = 16
MIX_W = D_MODEL
CONV_W = MIX_W // 4
CONV_K = 3
MLA_HEADS = 8
QK_NOPE = 64
QK_ROPE = 32
V_DIM = 64
MLA_W = MLA_HEADS * V_DIM
Q_LORA = 256
KV_LORA = 128
RWKV_HEAD = 64
RWKV_W = MIX_W - CONV_W - MLA_W
RWKV_HEADS = RWKV_W // RWKV_HEAD
DECAY_LORA = 64
ICLR_LORA = 64
SHIFT_W = 3 * RWKV_W + DECAY_LORA + ICLR_LORA
IN_SPLITS = (CONV_W, CONV_W, CONV_W, CONV_W, Q_LORA, KV_LORA, QK_ROPE, MLA_W, SHIFT_W, RWKV_W)
IN_TOTAL = sum(IN_SPLITS)
IN_OFFSETS = tuple(np.cumsum(IN_SPLITS)[:-1].tolist())
RWKV_OFFSETS = (RWKV_W, 2 * RWKV_W, 3 * RWKV_W, 3 * RWKV_W + DECAY_LORA)
ROPE_BASE = 10000.0
RMS_EPS = 1e-6
GN_EPS = 64e-5
QBLOCK = 128
NEG = -1e30
FAR_CHUNK = 2 ** 30

kernel_name = 'hybrid_conv_mla_rwkv7_stream_step'


def rmsnorm(x, g):
    xf = x.astype(jnp.float32)
    y = xf * lax.rsqrt(jnp.mean(xf * xf, axis=-1, keepdims=True) + RMS_EPS)
    return (y * g.astype(jnp.float32)).astype(x.dtype)


def rope(x, pos):
    half = x.shape[-1] // 2
    inv = ROPE_BASE ** (-jnp.arange(half, dtype=jnp.float32) * 2.0 / x.shape[-1])
    ang = pos.astype(jnp.float32)[:, None] * inv[None, :]
    cos = jnp.cos(ang)[None, :, None, :]
    sin = jnp.sin(ang)[None, :, None, :]
    xf = x.astype(jnp.float32)
    x1, x2 = xf[..., :half], xf[..., half:]
    return jnp.concatenate([x1 * cos - x2 * sin, x1 * sin + x2 * cos], axis=-1).astype(x.dtype)


def chunk_attention(q_nope, q_rope, k_nope, k_rope, v, q_chunk, k_chunk):
    scale = (QK_NOPE + QK_ROPE) ** -0.5

    def block(args):
        qn, qr, qc = args
        s = (jnp.einsum('bqhd,bkhd->bhqk', qn, k_nope).astype(jnp.float32)
             + jnp.einsum('bqhr,bkr->bhqk', qr, k_rope).astype(jnp.float32)) * scale
        vis = k_chunk[None, :] <= qc[:, None]
        s = jnp.where(vis[None, None], s, NEG)
        p = jax.nn.softmax(s, axis=-1).astype(v.dtype)
        return jnp.einsum('bhqk,bkhd->bqhd', p, v)

    b, sq = q_nope.shape[:2]
    if sq <= QBLOCK:
        return block((q_nope, q_rope, q_chunk))
    nb = -(-sq // QBLOCK)
    pad = nb * QBLOCK - sq

    def blocks(t):
        t = jnp.pad(t, ((0, 0), (0, pad), (0, 0), (0, 0)))
        return jnp.moveaxis(t.reshape(b, nb, QBLOCK, *t.shape[2:]), 1, 0)

    qc = jnp.pad(q_chunk, (0, pad), constant_values=FAR_CHUNK).reshape(nb, QBLOCK)
    out = lax.map(block, (blocks(q_nope), blocks(q_rope), qc))
    out = jnp.moveaxis(out, 0, 1).reshape(b, nb * QBLOCK, *out.shape[3:])
    return out[:, :sq]


def rwkv7_scan(r, w, k, v, kk, a, s0):
    def step(s, inp):
        r_t, w_t, k_t, v_t, kk_t, a_t = inp
        sa = jnp.einsum('bhvk,bhk->bhv', s, -kk_t)
        s = (s * w_t[:, :, None, :] + sa[..., None] * (kk_t * a_t)[:, :, None, :]
             + v_t[..., None] * k_t[:, :, None, :])
        return s, jnp.einsum('bhvk,bhk->bhv', s, r_t)

    xs = tuple(jnp.moveaxis(t, 1, 0) for t in (r, w, k, v, kk, a))
    s, ys = lax.scan(step, s0, xs)
    return jnp.moveaxis(ys, 0, 1), s


def hybrid_layer(x, pos, q_chunk, past_chunk, ckv_past, krope_past, conv_st, shift_st, wkv_st,
                 norm_g, w_in, conv_w, q_norm_g, w_uq, kv_norm_g, w_ukv, shift_mu,
                 decay_w0, decay_w2, iclr_a0, iclr_a2, key_kk, key_ka, bonus_rk,
                 lnx_w, lnx_b, w_out):
    b, t = x.shape[:2]
    h = rmsnorm(x, norm_g)
    z = h @ w_in
    xin, bg, cg, ga, cq, ckv, kr, gb, zc, gc = jnp.split(z, IN_OFFSETS, axis=-1)

    u = cg * xin
    u_ext = jnp.concatenate([conv_st.astype(u.dtype), u], axis=1)
    conv = sum(conv_w[j] * u_ext[:, j:j + t] for j in range(CONV_K))
    y_a = bg * conv * jax.nn.silu(ga)
    new_conv = u_ext[:, -(CONV_K - 1):]

    q = (rmsnorm(cq, q_norm_g) @ w_uq).reshape(b, t, MLA_HEADS, QK_NOPE + QK_ROPE)
    q_nope = q[..., :QK_NOPE]
    q_rope = rope(q[..., QK_NOPE:], pos)
    ckv_n = rmsnorm(ckv, kv_norm_g)
    kr_r = rope(kr[:, :, None, :], pos)[:, :, 0]
    ckv_all = jnp.concatenate([ckv_past.astype(ckv_n.dtype), ckv_n], axis=1)
    kr_all = jnp.concatenate([krope_past.astype(kr_r.dtype), kr_r], axis=1)
    sk = ckv_all.shape[1]
    kv = (ckv_all @ w_ukv).reshape(b, sk, MLA_HEADS, QK_NOPE + V_DIM)
    k_chunk = jnp.concatenate([past_chunk, q_chunk])
    o = chunk_attention(q_nope, q_rope, kv[..., :QK_NOPE], kr_all, kv[..., QK_NOPE:],
                        q_chunk, k_chunk)
    y_b = o.reshape(b, t, MLA_W) * jax.nn.silu(gb)

    prev = jnp.concatenate([shift_st[:, None].astype(zc.dtype), zc[:, :-1]], axis=1)
    zs = zc + (prev - zc) * shift_mu
    r, k, v, wl, al = jnp.split(zs, RWKV_OFFSETS, axis=-1)
    w_log = -jax.nn.softplus(-(decay_w0 + jnp.tanh(wl) @ decay_w2).astype(jnp.float32)) - 0.5
    decay = jnp.exp(-jnp.exp(w_log))
    a = jax.nn.sigmoid((iclr_a0 + al @ iclr_a2).astype(jnp.float32))
    heads = lambda m: m.astype(jnp.float32).reshape(b, t, RWKV_HEADS, RWKV_HEAD)
    per_head = lambda p: p.astype(jnp.float32).reshape(RWKV_HEADS, RWKV_HEAD)
    r, k, v, decay, a = heads(r), heads(k), heads(v), heads(decay), heads(a)
    kk = k * per_head(key_kk)
    kk = kk / jnp.maximum(jnp.sqrt(jnp.sum(kk * kk, axis=-1, keepdims=True)), 1e-12)
    k = k * (1.0 + (a - 1.0) * per_head(key_ka))
    ys, s_new = rwkv7_scan(r, decay, k, v, kk, a, wkv_st.astype(jnp.float32))
    mu = jnp.mean(ys, axis=-1, keepdims=True)
    var = jnp.mean(jnp.square(ys - mu), axis=-1, keepdims=True)
    yn = (ys - mu) * lax.rsqrt(var + GN_EPS) * per_head(lnx_w) + per_head(lnx_b)
    yn = yn + jnp.sum(r * k * per_head(bonus_rk), axis=-1, keepdims=True) * v
    y_c = yn.reshape(b, t, RWKV_W).astype(x.dtype) * jax.nn.silu(gc)
    new_shift = zc[:, -1]

    y = jnp.concatenate([y_a, y_b, y_c], axis=-1) @ w_out
    return x + y, ckv_n, kr_r, new_conv, new_shift, s_new.astype(wkv_st.dtype)


def setup_inputs(seed: int = 0) -> dict:
    key = jax.random.key(seed)
    ks = jax.random.split(key, 32)
    f32 = jnp.float32
    nrm = lambda kk, shape, s: jax.random.normal(kk, shape, f32) * s
    return {
        'x_prompt': nrm(ks[0], (BATCH, SEQ, D_MODEL), 1.0),
        'x_sample': nrm(ks[1], (DEC_BATCH, DEC_SEQ, D_MODEL), 1.0),
        'cache_ckv': nrm(ks[2], (DEPTH, DEC_BATCH, PAST_LEN, KV_LORA), 1.0),
        'cache_krope': nrm(ks[3], (DEPTH, DEC_BATCH, PAST_LEN, QK_ROPE), 1.0),
        'state_conv': nrm(ks[4], (DEPTH, DEC_BATCH, CONV_K - 1, CONV_W), 1.0),
        'state_shift': nrm(ks[5], (DEPTH, DEC_BATCH, SHIFT_W), 1.0),
        'state_wkv': nrm(ks[6], (DEPTH, DEC_BATCH, RWKV_HEADS, RWKV_HEAD, RWKV_HEAD), 0.5),
        'meta_tokens': nrm(ks[7], (N_META, D_MODEL), 1.0),
        'norm_g': 1.0 + nrm(ks[8], (DEPTH, D_MODEL), 0.02),
        'w_in': nrm(ks[9], (DEPTH, D_MODEL, IN_TOTAL), D_MODEL ** -0.5),
        'conv_w': nrm(ks[10], (DEPTH, CONV_K, CONV_W), CONV_K ** -0.5),
        'q_norm_g': 1.0 + nrm(ks[11], (DEPTH, Q_LORA), 0.02),
        'w_uq': nrm(ks[12], (DEPTH, Q_LORA, MLA_HEADS * (QK_NOPE + QK_ROPE)), Q_LORA ** -0.5),
        'kv_norm_g': 1.0 + nrm(ks[13], (DEPTH, KV_LORA), 0.02),
        'w_ukv': nrm(ks[14], (DEPTH, KV_LORA, MLA_HEADS * (QK_NOPE + V_DIM)), KV_LORA ** -0.5),
        'shift_mu': jax.random.uniform(ks[15], (DEPTH, SHIFT_W), f32),
        'decay_w0': -1.0 + nrm(ks[16], (DEPTH, RWKV_W), 0.5),
        'decay_w2': nrm(ks[17], (DEPTH, DECAY_LORA, RWKV_W), 0.1 * DECAY_LORA ** -0.5),
        'iclr_a0': nrm(ks[18], (DEPTH, RWKV_W), 0.1),
        'iclr_a2': nrm(ks[19], (DEPTH, ICLR_LORA, RWKV_W), 0.1 * ICLR_LORA ** -0.5),
        'key_kk': 0.85 + nrm(ks[20], (DEPTH, RWKV_W), 0.05),
        'key_ka': 1.0 + nrm(ks[21], (DEPTH, RWKV_W), 0.05),
        'bonus_rk': nrm(ks[22], (DEPTH, RWKV_W), 0.1),
        'lnx_w': 1.0 + nrm(ks[23], (DEPTH, RWKV_W), 0.02),
        'lnx_b': nrm(ks[24], (DEPTH, RWKV_W), 0.02),
        'w_out': nrm(ks[25], (DEPTH, MIX_W, D_MODEL), (2.0 * DEPTH * MIX_W) ** -0.5),
        'final_g': 1.0 + nrm(ks[26], (D_MODEL,), 0.02),
    }


def stack_states(outs):
    return tuple(jnp.stack([o[i] for o in outs]) for i in range(5))


def reference(x_prompt, x_sample, cache_ckv, cache_krope, state_conv, state_shift, state_wkv,
              meta_tokens, norm_g, w_in, conv_w, q_norm_g, w_uq, kv_norm_g, w_ukv, shift_mu,
              decay_w0, decay_w2, iclr_a0, iclr_a2, key_kk, key_ka, bonus_rk, lnx_w, lnx_b,
              w_out, final_g):
    dt = x_prompt.dtype
    bp = x_prompt.shape[0]
    meta = jnp.broadcast_to(meta_tokens.astype(dt)[None], (bp, N_META, D_MODEL))
    hp = jnp.concatenate([meta, x_prompt], axis=1)
    tp = hp.shape[1]
    pos_p = jnp.arange(tp, dtype=jnp.int32)
    chunk_p = jnp.where(pos_p < N_META, -1, (pos_p - N_META) // CHUNK).astype(jnp.int32)
    past_chunk_p = jnp.zeros((0,), jnp.int32)
    ckv0 = jnp.zeros((bp, 0, KV_LORA), dt)
    kr0 = jnp.zeros((bp, 0, QK_ROPE), dt)
    conv0 = jnp.zeros((bp, CONV_K - 1, CONV_W), dt)
    shift0 = jnp.zeros((bp, SHIFT_W), dt)
    wkv0 = jnp.zeros((bp, RWKV_HEADS, RWKV_HEAD, RWKV_HEAD), dt)

    hs = x_sample
    ts = x_sample.shape[1]
    past = cache_ckv.shape[2]
    pos_s = past + jnp.arange(ts, dtype=jnp.int32)
    chunk_s = jnp.full((ts,), past // CHUNK, jnp.int32)
    past_chunk_s = jnp.arange(past, dtype=jnp.int32) // CHUNK

    outs_p, outs_s = [], []
    for l in range(DEPTH):
        lw = tuple(p[l] for p in (norm_g, w_in, conv_w, q_norm_g, w_uq, kv_norm_g, w_ukv,
                                   shift_mu, decay_w0, decay_w2, iclr_a0, iclr_a2, key_kk,
                                   key_ka, bonus_rk, lnx_w, lnx_b, w_out))
        hp, *st_p = hybrid_layer(hp, pos_p, chunk_p, past_chunk_p, ckv0, kr0, conv0, shift0,
                                 wkv0, *lw)
        hs, *st_s = hybrid_layer(hs, pos_s, chunk_s, past_chunk_s, cache_ckv[l], cache_krope[l],
                                 state_conv[l], state_shift[l], state_wkv[l], *lw)
        outs_p.append(st_p)
        outs_s.append(st_s)

    y_prompt = rmsnorm(hp[:, N_META:], final_g)
    y_sample = rmsnorm(hs, final_g)
    ckv_p, kr_p, conv_p, shift_p, wkv_p = stack_states(outs_p)
    ckv_s, kr_s, conv_s, shift_s, wkv_s = stack_states(outs_s)
    return (y_prompt, y_sample, ckv_p, kr_p, conv_p, shift_p, wkv_p,
            ckv_s, kr_s, conv_s, shift_s, wkv_s)
```

```python
import contextlib
import numpy as np
import concourse.bass as bass
import concourse.mybir as mybir
from concourse.bass_utils import run_bass_kernel_spmd

F32 = mybir.dt.float32
BF16 = mybir.dt.bfloat16
F32R = mybir.dt.float32r


def R(ap):
    return ap.bitcast(F32R)

ALU = mybir.AluOpType
AF = mybir.ActivationFunctionType
AX = mybir.AxisListType

ENGS = ("pe", "act", "dve", "pool", "sp")


class Op:
    __slots__ = ("eng", "fn", "deps", "idx", "needs_inc", "cnt", "dma", "dsem", "dval", "rb")

    def __init__(self, eng, fn, dma):
        self.eng = eng
        self.fn = fn
        self.deps = []
        self.idx = -1
        self.needs_inc = False
        self.cnt = 0
        self.dma = dma
        self.dsem = None
        self.dval = 0


class Sched:
    NDSEM = 12

    def __init__(self, nc):
        self.nc = nc
        self.q = {e: [] for e in ENGS}
        self.lastw = {}
        self.readers = {}
        self.dma_rr = {e: 0 for e in ENGS}
        self.dma_last = {}
        self.dma_cnt = {}

    def add(self, eng, fn, reads=(), writes=(), dma=False, serial=False, rb=0):
        import os, sys
        self.nadd = getattr(self, "nadd", 0) + 1
        mx = int(os.environ.get("KMAXOPS", "0"))
        if mx and self.nadd > mx:
            return None
        if mx and self.nadd > mx - 3:
            f = sys._getframe(1)
            if f.f_code.co_name in ("DMA", "MM", "TR", "cast_scale"):
                f = f.f_back
            print("OP", self.nadd, eng, f.f_code.co_name, f.f_lineno, flush=True)
        op = Op(eng, fn, dma)
        op.rb = rb
        deps = {}
        pk = [r for r in reads if isinstance(r, str) and r.startswith("ps")]
        if pk:
            reads = [r for r in reads if r not in pk]
            writes = list(writes) + [r for r in pk if r not in writes]
        for r in reads:
            w = self.lastw.get(r)
            if w is not None:
                deps[id(w)] = w
        for wkey in writes:
            w = self.lastw.get(wkey)
            if w is not None:
                deps[id(w)] = w
            for rd in self.readers.get(wkey, ()):
                deps[id(rd)] = rd
        for wkey in writes:
            self.lastw[wkey] = op
            self.readers[wkey] = []
        for r in reads:
            self.readers.setdefault(r, []).append(op)
        if dma:
            slot = self.dma_rr[eng] % self.NDSEM
            self.dma_rr[eng] += 1
            key = (eng, slot)
            prev = self.dma_last.get(key)
            if prev is not None:
                deps[id(prev)] = prev
            self.dma_last[key] = op
            self.dma_cnt[key] = self.dma_cnt.get(key, 0) + 1
            op.dsem = key
            op.dval = 16 * self.dma_cnt[key]
        op.idx = len(self.q[eng])
        if serial and self.q[eng]:
            pv = self.q[eng][-1]
            op.deps.append(pv)
            pv.needs_inc = True
        self.q[eng].append(op)
        for d in deps.values():
            if d is op:
                continue
            if d.dma:
                op.deps.append(d)
            elif d.eng == eng:
                if dma:
                    op.deps.append(d)
                    d.needs_inc = True
                elif eng == "pe":
                    if d.rb != rb:
                        op.deps.append(d)
                        d.needs_inc = True
                    continue
                elif op.idx - d.idx <= 2:
                    op.deps.append(d)
                    d.needs_inc = True
            else:
                op.deps.append(d)
                d.needs_inc = True
        return op

    def emit(self):
        nc = self.nc
        with contextlib.ExitStack() as st:
            esem = {e: st.enter_context(nc.semaphore(f"s_{e}")) for e in ENGS}
            dsem = {}
            for e in ENGS:
                for s in range(min(self.NDSEM, self.dma_rr[e])):
                    dsem[(e, s)] = st.enter_context(nc.semaphore(f"d_{e}{s}"))
            for e in ENGS:
                c = 0
                for op in self.q[e]:
                    if op.needs_inc and not op.dma:
                        c += 1
                        op.cnt = c
            block = st.enter_context(nc.Block())

            def run(engname, eng):
                seen = {}
                for op in self.q[engname]:
                    for d in op.deps:
                        if d.dma:
                            sem, val = dsem[d.dsem], d.dval
                        else:
                            sem, val = esem[d.eng], d.cnt
                        k = id(sem)
                        if seen.get(k, 0) >= val:
                            continue
                        seen[k] = val
                        eng.wait_ge(sem, val)
                    ins = op.fn(eng)
                    if op.dma:
                        ins.then_inc(dsem[op.dsem], 16)
                    elif op.needs_inc:
                        ins.then_inc(esem[engname], 1)
                for (e, s), last in self.dma_last.items():
                    if e == engname:
                        eng.wait_ge(dsem[(e, s)], last.dval)

            if self.q["pe"]:
                @block.tensor
                def _(eng):
                    run("pe", eng)
            if self.q["act"]:
                @block.scalar
                def _(eng):
                    run("act", eng)
            if self.q["dve"]:
                @block.vector
                def _(eng):
                    run("dve", eng)
            if self.q["pool"]:
                @block.gpsimd
                def _(eng):
                    run("pool", eng)
            if self.q["sp"]:
                @block.sync
                def _(eng):
                    run("sp", eng)


D = 1024
IN_TOTAL = 3104
NV = 33
SCALE = float(96 ** -0.5)
LOGW_C = float(-np.exp(-0.5))
RMS_EPS = 1e-6
GN_EPS = 64e-5


def build_nc(TP, NS, PAST):
    nc = bass.Bass("TRN2", target_bir_lowering=False)
    NPT = 1 + (TP - 16) // 128
    NKT_S = PAST // 128 + 1
    NPOS = TP + 64

    def din(name, shape):
        return nc.dram_tensor(name, list(shape), F32, kind="ExternalInput").ap()

    def dout(name, shape):
        return nc.dram_tensor(name, list(shape), F32, kind="ExternalOutput").ap()

    xp = din("xp", [TP, D]); xs = din("xs", [NS * 64, D])
    cckv = din("cckv", [2, NS, PAST, 128]); ckr = din("ckr", [2, NS, PAST, 32])
    sconvT = din("sconvT", [2, NS, 256, 2]); sshiftT = din("sshiftT", [2, NS, 128, 7])
    swkv = din("swkv", [2, NS, 256, 64])
    w_in = din("w_in", [2, D, IN_TOTAL]); w_out = din("w_out", [2, D, D])
    w_uq = din("w_uq", [2, 256, 768]); w_ukv = din("w_ukv", [2, 128, 1024])
    dw2 = din("dw2", [2, 64, 256]); a2 = din("a2", [2, 64, 256])
    vecs = din("vecs", [2, 128, NV])
    kvg = din("kvg", [2, 1, 128]); lnw = din("lnw", [2, 1, 256]); lnb = din("lnb", [2, 1, 256])
    fg = din("fg", [1, D])
    c_ident = din("c_ident", [128, 128]); c_blk = din("c_blk", [128, 128])
    c_mg = din("c_mg", [128, 416]); c_sl = din("c_sl", [128, 128]); c_am = din("c_am", [128, 128])
    c_scan = din("c_scan", [128, 256])
    rope_tm = din("rope_tm", [NPOS, 32]); rope_fm = din("rope_fm", [32, 2, NPOS])

    y_p = dout("y_p", [TP - 16, D]); y_s = dout("y_s", [NS * 64, D])
    ckv_p = dout("ckv_p", [2, TP, 128]); kr_p = dout("kr_p", [2, TP, 32])
    conv_p = dout("conv_p", [2, 256, 2]); shift_p = dout("shift_p", [2, 128, 7]); wkv_p = dout("wkv_p", [2, 256, 64])
    ckv_s = dout("ckv_s", [2, NS, 64, 128]); kr_s = dout("kr_s", [2, NS, 64, 32])
    conv_s = dout("conv_s", [2, NS, 256, 2]); shift_s = dout("shift_s", [2, NS, 128, 7]); wkv_s = dout("wkv_s", [2, NS, 256, 64])
    scr_p = nc.dram_tensor("scr_p", [TP, D], F32, kind="Internal").ap()
    scr_s = nc.dram_tensor("scr_s", [NS * 64, D], F32, kind="Internal").ap()

    with contextlib.ExitStack() as st:
        def sb(name, shape, dt=F32):
            return st.enter_context(nc.sbuf_tensor(name, list(shape), dt))

        def pst(name):
            return st.enter_context(nc.psum_tensor(name, [128, 512], F32))

        S = Sched(nc)
        defer = [None]

        def A(*a_, **k_):
            if defer[0] is not None:
                defer[0].append((a_, k_))
                return None
            return S.add(*a_, **k_)
        PS = [pst(f"ps{i}") for i in range(8)]

        W_in = sb("W_in", [128, 8, IN_TOTAL], BF16)
        W_out = sb("W_out", [128, 8, D], BF16)
        W_uq = sb("W_uq", [128, 2, 768], BF16)
        W_uqsw = sb("W_uqsw", [128, 2, 256], BF16)
        W_ukT = sb("W_ukT", [64, 8, 128], BF16)
        W_uvp = sb("W_uvp", [128, 8, 128], BF16)
        dwa = sb("dwa", [128, 256])
        vec = sb("vec", [128, NV]); omka = sb("omka", [128, 2]); bonmat = sb("bonmat", [128, 2, 4])
        kvg_bc = sb("kvg_bc", [128, 128]); lnw_bc = sb("lnw_bc", [128, 256]); lnb_bc = sb("lnb_bc", [128, 256])
        fg_bc = sb("fg_bc", [128, D])
        ident = sb("ident", [128, 128]); identr = sb("identr", [128, 128]); identb = sb("identb", [128, 128], BF16); hbf = sb("hbf", [128, D], BF16); ycb = sb("ycb", [128, 256], BF16); blk = sb("blk", [128, 128]); mg = sb("mg", [128, 416]); msl = sb("msl", [128, 128])
        amf = sb("amf", [128, 128]); amask = sb("amask", [128, 128], BF16); scanm = sb("scanm", [128, 256])
        ones_bf = sb("ones_bf", [128, 128], BF16)
        stage = sb("stage", [128, 2, 1024])
        CKVT_p = sb("CKVT_p", [128, TP], BF16); CKV_p = sb("CKV_p", [128, NPT, 129], BF16); KRT_p = sb("KRT_p", [32, TP], BF16)
        CKVT_s = sb("CKVT_s", [128, PAST], BF16); CKV_s = sb("CKV_s", [128, PAST // 128, 129], BF16)
        KRT_s = sb("KRT_s", [32, PAST], BF16)
        CKVT_o = sb("CKVT_o", [128, 2, 64], BF16); CKV_o = sb("CKV_o", [64, 2, 129], BF16); KRT_o = sb("KRT_o", [32, 2, 64], BF16)
        ux = sb("ux", [128, 2, 132]); uxcar = sb("uxcar", [128, 3, 2, 2]); zcx = sb("zcx", [128, 7, 130]); zcar = sb("zcar", [128, 3, 7])
        Hp = sb("Hp", [128, 3, 2, 128])
        xt = sb("xt", [128, 2, D]); hfb = sb("hfb", [128, 2, D]); hT = sb("hT", [128, 8, 128], BF16)
        catT = sb("catT", [128, 8, 128], BF16)
        ss = sb("ss", [128, 8]); rs_ = sb("rs_", [128, 8]); epsb = sb("epsb", [128, 2])
        cqn = sb("cqn", [128, 256]); cqnT = sb("cqnT", [128, 2, 128], BF16)
        ckvn = sb("ckvn", [128, 128]); krr = sb("krr", [128, 32]); ropeA = sb("ropeA", [128, 32]); ropeB = sb("ropeB", [128, 32])
        cs_tm2 = sb("cs_tm2", [128, 2, 32]); cs_fm2 = sb("cs_fm2", [32, 2, 2, 128])
        sgc = sb("sgc", [128, 2, 256])
        xin_sb = sb("xin_sb", [128, 128]); cv1 = sb("cv1", [128, 128]); sga = sb("sga", [128, 128])
        sgb = sb("sgb", [128, 4, 128])
        zs = sb("zs", [128, 7, 128])
        QRT = sb("QRT", [32, 8, 128], BF16); QaT = sb("QaT", [128, 8, 128], BF16)
        PT = sb("PT", [128, 2, 512], BF16); yraw = sb("yraw", [128, 256]); rden = sb("rden", [128, 2, 4]); otok3 = sb("otok3", [128, 2, 3, 128], BF16); OaT = sb("OaT", [128, 8, 128], BF16)
        QnT = OaT
        th = sb("th", [64, 128]); lw = sb("lw", [128, 2, 128]); av = sb("av", [128, 2, 128]); G = sb("G", [128, 2, 128])
        Epos = sb("Epos", [128, 2, 128]); Eneg = sb("Eneg", [128, 2, 128]); Eprev = sb("Eprev", [128, 2, 128])
        kk = sb("kk", [128, 2, 128]); kmod = sb("kmod", [128, 2, 128]); bb = sb("bb", [128, 2, 128])
        rt1 = Eprev
        AR = sb("AR", [128, 2, 256], BF16); BTb = sb("BTb", [128, 2, 128], BF16); KTb = sb("KTb", [128, 2, 128], BF16); Vtokb = sb("Vtokb", [128, 2, 256], BF16); Hb = sb("Hb", [128, 128], BF16); BT = sb("BT", [128, 2, 128]); KT = sb("KT", [128, 2, 128]); rkp = G
        rkb = sb("rkb", [128, 2, 4])
        Btok = sb("Btok", [128, 2, 256], BF16); Ktok = sb("Ktok", [128, 2, 256], BF16); Vtok = sb("Vtok", [128, 2, 256])
        Gm = sb("Gm", [128, 2, 512], BF16)
        NX = sb("NX", [128, 2, 2, 128], BF16); NXT = sb("NXT", [128, 2, 2, 128], BF16); NW = sb("NW", [128, 2, 2, 128], BF16)
        X1 = sb("X1", [128, 128], BF16); U = sb("U", [128, 128], BF16)
        yc = sb("yc", [128, 256]); gst = sb("gst", [128, 8]); ybv = sb("ybv", [128, 256])
        ysq = ybv
        Sp = sb("Sp", [128, 128]); wko = sb("wko", [128, 128])

        dq = [0]

        dma_sp_only = [False]

        def DMA(out, in_, reads=(), writes=()):
            e = "sp" if (dq[0] % 2 == 0 or dma_sp_only[0]) else "pool"
            dq[0] += 1
            A(e, lambda g, o=out, i=in_: g.dma_start(out=o, in_=i), reads=reads, writes=writes, dma=True)

        def MM(ps_key, mms, reads):
            def fn(g, mms=mms):
                ins = None
                for i, (o, l, r) in enumerate(mms):
                    ins = g.matmul(o, lhsT=l, rhs=r, start=(i == 0), stop=(i == len(mms) - 1))
                return ins
            A("pe", fn, reads=reads, writes=[ps_key])

        def TR(ps_key, outs_ins, reads, bf=False):
            idt_ = identb if bf else ident

            def fn(g, oi=outs_ins):
                ins = None
                for (o, i_) in oi:
                    ins = g.transpose(o, i_, idt_[0:i_.shape[0], 0:i_.shape[0]])
                return ins
            A("pe", fn, reads=list(reads) + ["ident"], writes=[ps_key])

        def PSB(i):
            return PS[i][:].bitcast(BF16)

        def PE1(out, lhsT, rhs, start, stop, reads, writes, r32=False, rb=0):
            if r32:
                lhsT, rhs = R(lhsT), R(rhs)
            A("pe", lambda g: g.matmul(out, lhsT=lhsT, rhs=rhs, start=start, stop=stop), reads=reads, writes=writes, rb=rb)

        DMA(ident[:], c_ident[:, :], writes=["ident"]); DMA(blk[:], c_blk[:, :], writes=["blk"])
        DMA(mg[:], c_mg[:, :], writes=["mg"]); DMA(msl[:], c_sl[:, :], writes=["msl"])
        DMA(amf[:], c_am[:, :], writes=["amf"]); DMA(scanm[:], c_scan[:, :], writes=["scanm"])
        DMA(fg_bc[:], fg.partition_broadcast(128), writes=["fg_bc"])
        A("dve", lambda g: g.tensor_copy(amask[:], amf[:]), reads=["amf"], writes=["amask"])
        A("dve", lambda g: g.tensor_copy(R(identr[:]), ident[:]), reads=["ident"], writes=["identr"])
        A("dve", lambda g: g.tensor_copy(identb[:], ident[:]), reads=["ident"], writes=["ident"])
        A("pool", lambda g: g.memset(ones_bf[:], 1.0), writes=["ones_bf"])
        A("pool", lambda g: g.memset(CKV_p[:], 1.0), writes=["kvp"])
        A("pool", lambda g: g.memset(CKV_s[:], 1.0), writes=["kvs"])
        A("pool", lambda g: g.memset(CKV_o[:], 1.0), writes=["kvo0", "kvo1"])

        cast_rr = [0]

        def cast_scale(out, in_, scal, reads, writes):
            i = (0, 2)[cast_rr[0] % 2]
            cast_rr[0] += 1
            if scal is None:
                if i == 0:
                    A("dve", lambda g: g.tensor_copy(out, in_), reads=reads, writes=writes)
                elif i == 1:
                    A("pool", lambda g: g.tensor_copy(out, in_), reads=reads, writes=writes)
                else:
                    A("act", lambda g: g.copy(out, in_), reads=reads, writes=writes)
            else:
                if i == 0:
                    A("dve", lambda g: g.tensor_scalar(out, in_, scal, None, ALU.mult), reads=reads, writes=writes)
                elif i == 1:
                    A("pool", lambda g: g.tensor_scalar(out, in_, scal, None, ALU.mult), reads=reads, writes=writes)
                else:
                    A("act", lambda g: g.activation(out, in_, AF.Copy, scale=scal), reads=reads, writes=writes)

        stg = [0]

        def stage_load(src, ncols):
            b = stg[0] % 2
            stg[0] += 1
            DMA(stage[:, b, 0:ncols], src, writes=[f"stage{b}"])
            return b

        def load_weights(l):
            dma_sp_only[0] = True
            DMA(vec[:], vecs[l], writes=["vec"])
            DMA(kvg_bc[:], kvg[l].partition_broadcast(128), writes=["kvg_bc"])
            DMA(lnw_bc[:], lnw[l].partition_broadcast(128), writes=["lnw_bc"])
            DMA(lnb_bc[:], lnb[l].partition_broadcast(128), writes=["lnb_bc"])
            DMA(dwa[0:64, :], dw2[l], writes=["dwa"]); DMA(dwa[64:128, :], a2[l], writes=["dwa"])
            A("dve", lambda g: g.tensor_scalar(omka[:], vec[:, 29:31], -1.0, 1.0, ALU.mult, ALU.add), reads=["vec"], writes=["omka"])
            A("pool", lambda g: g.memset(bonmat[:], 0.0), writes=["bonmat"])
            for c in range(2):
                A("pool", lambda g, c=c: g.tensor_copy(bonmat[0:64, c, 2 * c:2 * c + 1], vec[0:64, 31 + c:32 + c]), reads=["vec"], writes=["bonmat"])
                A("pool", lambda g, c=c: g.tensor_copy(bonmat[64:128, c, 2 * c + 1:2 * c + 2], vec[64:128, 31 + c:32 + c]), reads=["vec"], writes=["bonmat"])
            for k in range(8):
                for c0 in range(0, IN_TOTAL, 1024):
                    nco = min(1024, IN_TOTAL - c0)
                    b = stage_load(w_in[l, k * 128:(k + 1) * 128, c0:c0 + nco], nco)
                    cast_scale(W_in[:, k, c0:c0 + nco], stage[:, b, 0:nco], vec[:, k:k + 1], [f"stage{b}", "vec"], ["W_in"])
            for k in range(8):
                b = stage_load(w_out[l, k * 128:(k + 1) * 128, :], 1024)
                cast_scale(W_out[:, k, :], stage[:, b, 0:1024], None, [f"stage{b}"], ["W_out"])
            for k in range(2):
                b = stage_load(w_uq[l, k * 128:(k + 1) * 128, :], 768)
                cast_scale(W_uq[:, k, :], stage[:, b, 0:768], vec[:, 8 + k:9 + k], [f"stage{b}", "vec"], ["W_uq"])
            wv = W_uq[:].rearrange("p k (h e) -> p k h e", h=8)
            wsw = W_uqsw[:].rearrange("p k (h e) -> p k h e", h=8)
            for k in range(2):
                A("dve", lambda g, k=k: g.tensor_scalar(wsw[:, k, :, 0:16], wv[:, k, :, 80:96], -1.0, None, ALU.mult), reads=["W_uq"], writes=["W_uqsw"])
                A("pool", lambda g, k=k: g.tensor_copy(wsw[:, k, :, 16:32], wv[:, k, :, 64:80]), reads=["W_uq"], writes=["W_uqsw"])
            b = stage_load(w_ukv[l], 1024)
            A("pool", lambda g: g.memset(W_uvp[:], 0.0), writes=["W_uvp"])
            for h in range(8):
                o = 64 * (h % 2)
                cast_scale(W_uvp[:, h, o:o + 64], stage[:, b, h * 128 + 64:h * 128 + 128], None, [f"stage{b}"], ["W_uvp"])
            for hh in range(2):
                TR("ps0", [(PS[0][:, j * 128:(j + 1) * 128], stage[:, b, (hh * 4 + j) * 128:(hh * 4 + j + 1) * 128]) for j in range(4)], [f"stage{b}"])
                A("dve", lambda g, hh=hh: g.tensor_copy(W_ukT[:, hh * 4:hh * 4 + 4, :], PS[0][0:64, :].rearrange("p (j e) -> p j e", j=4)), reads=["ps0"], writes=["W_ukT"])
            dma_sp_only[0] = False

        def rstd_from_ss(n, col, dim, eps):
            ec = {RMS_EPS: 0, GN_EPS: 1}[eps]
            A("act", lambda g: g.activation(rs_[0:n, col:col + 1], ss[0:n, col:col + 1], AF.Ln, bias=epsb[0:n, ec:ec + 1], scale=1.0 / dim), reads=["ss", "epsb"], writes=["rs_"])
            A("act", lambda g: g.activation(rs_[0:n, col:col + 1], rs_[0:n, col:col + 1], AF.Exp, scale=-0.5), reads=["rs_"], writes=["rs_"])

        def load_rope(t):
            par = t["par"]
            sg0 = t["segs"][0]
            C = sg0["C"]
            sv = dma_sp_only[0]
            dma_sp_only[0] = True
            DMA(cs_tm2[0:C, par, :], rope_tm[sg0["pos0"]:sg0["pos0"] + C, :], writes=[f"cs_tm{par}"])
            for sg in t["segs"]:
                DMA(cs_fm2[:, par, :, sg["col0"]:sg["col0"] + sg["C"]], rope_fm[:, :, sg["pos0"]:sg["pos0"] + sg["C"]], writes=[f"cs_fm{par}"])
            dma_sp_only[0] = sv

        def load_x(t):
            par = t["par"]
            rd = ["scr"] if t["l"] == 1 else []
            sv = dma_sp_only[0]
            dma_sp_only[0] = True
            DMA(xt[0:t["n"], par, :], t["x_src"], reads=rd, writes=[f"xt{par}"])
            dma_sp_only[0] = sv

        def front1a(t):
            n, par = t["n"], t["par"]
            X = xt[:, par, :]
            xk = f"xt{par}"
            hf = hfb[:, par, :]
            hk_ = f"hf{par}"
            A("dve", lambda g: g.memset(ss[:, 0:1], 0.0), writes=["ss"])
            A("act", lambda g: g.activation(hf[0:n, :], X[0:n, :], AF.Square, accum_out=ss[0:n, 0:1]), reads=[xk, "ss"], writes=[hk_, "ss"])
            rstd_from_ss(n, 0, D, RMS_EPS)
            A("dve", lambda g: g.tensor_scalar(hbf[0:n, :], X[0:n, :], rs_[0:n, 0:1], None, ALU.mult), reads=[xk, "rs_"], writes=["hbf"])

        def front1b(t):
            n, par = t["n"], t["par"]
            hf = hfb[:, par, :]
            hk_ = f"hf{par}"
            for hh in range(2):
                TR(f"ps{hh}", [(PSB(hh)[:, j * 128:j * 128 + n], hbf[0:n, (hh * 4 + j) * 128:(hh * 4 + j + 1) * 128]) for j in range(4)], ["hbf"], bf=True)
                src = PSB(hh)[:, 0:512].rearrange("p (j e) -> p j e", j=4)[:, :, 0:n]
                if hh == 0:
                    A("act", lambda g, src=src: g.copy(hT[:, 0:4, 0:n], src), reads=["ps0"], writes=["hT"])
                else:
                    A("dve", lambda g, src=src: g.tensor_copy(hT[:, 4:8, 0:n], src), reads=["ps1"], writes=["hT"])

        def do_tile(t):
            l, x_dst, y_dst, segs, n, par = t["l"], t["x_dst"], t["y_dst"], t["segs"], t["n"], t["par"]
            nseg = len(segs)
            X = xt[:, par, :]
            xk = f"xt{par}"
            C0 = segs[0]["C"]
            hf = hfb[:, par, :]
            hfk = f"hf{par}"
            cs_tm = cs_tm2[:, par, :]
            cs_fm = cs_fm2[:, par, :, :]
            ctk, cfk = f"cs_tm{par}", f"cs_fm{par}"
            qt1 = hf[0:32, 0:512].rearrange("p (j e) -> p j e", j=4)
            qt2 = hf[0:32, 512:1024].rearrange("p (j e) -> p j e", j=4)
            cst2 = hf[:, 0:256]

            def segview(ap2d):
                return ap2d.rearrange("p (s c) -> p s c", s=nseg)
            if not t.get('front_done'):
                front1a(t)
                front1b(t)
            def step2A(si, sg):
                c0, C = sg['col0'], sg['C']
                MM("ps2", [(PS[2][0:C, 0:416], hT[:, k, c0:c0 + C], W_in[:, k, 1024:1440]) for k in range(8)], ["hT", "W_in"])
                MM("ps3", [(PS[3][0:C, 0:256], hT[:, k, c0:c0 + C], W_in[:, k, 2848:3104]) for k in range(8)], ["hT", "W_in"])
                A("dve", lambda g: g.memset(ss[:, 1:3], 0.0), writes=["ss"])
                A("act", lambda g, C=C: g.activation(cst2[0:C, :], PS[2][0:C, 0:256], AF.Square, accum_out=ss[0:C, 1:2]), reads=["ps2", "ss"], writes=[hfk, "ss"])
                A("act", lambda g, C=C: g.activation(cst2[0:C, 0:128], PS[2][0:C, 256:384], AF.Square, accum_out=ss[0:C, 2:3]), reads=["ps2", "ss"], writes=[hfk, "ss"])
                rstd_from_ss(C, 1, 256, RMS_EPS)
                rstd_from_ss(C, 2, 128, RMS_EPS)
                A("dve", lambda g, C=C: g.tensor_scalar(cqn[0:C, :], PS[2][0:C, 0:256], rs_[0:C, 1:2], None, ALU.mult), reads=["ps2", "rs_"], writes=["cqn"])
                A("dve", lambda g, C=C: g.scalar_tensor_tensor(ckvn[0:C, :], PS[2][0:C, 256:384], rs_[0:C, 2:3], kvg_bc[0:C, :], ALU.mult, ALU.mult), reads=["ps2", "rs_", "kvg_bc"], writes=["ckvn"])
                krv = PS[2][0:C, 384:416].rearrange("p (t e) -> p t e", t=2)
                A("dve", lambda g, C=C, krv=krv: g.tensor_tensor(ropeA[0:C, :].rearrange("p (t e) -> p t e", t=2), krv, cs_tm[0:C, 0:16].unsqueeze(1).broadcast_to([C, 2, 16]), ALU.mult), reads=["ps2", ctk], writes=["ropeA"])
                A("dve", lambda g, C=C, krv=krv: g.tensor_tensor(ropeB[0:C, :].rearrange("p (t e) -> p t e", t=2), krv, cs_tm[0:C, 16:32].unsqueeze(1).broadcast_to([C, 2, 16]), ALU.mult), reads=["ps2", ctk], writes=["ropeB"])
                A("dve", lambda g, C=C: g.tensor_tensor(krr[0:C, 0:16], ropeA[0:C, 0:16], ropeB[0:C, 16:32], ALU.subtract), reads=["ropeA", "ropeB"], writes=["krr"])
                A("dve", lambda g, C=C: g.tensor_tensor(krr[0:C, 16:32], ropeB[0:C, 0:16], ropeA[0:C, 16:32], ALU.add), reads=["ropeA", "ropeB"], writes=["krr"])
                A("act", lambda g, C=C, si=si: g.activation(sgc[0:C, si, :], PS[3][0:C, 0:256], AF.Silu), reads=["ps3"], writes=["sgc"])
                DMA(sg["ckv_out"], ckvn[0:C, :], reads=["ckvn"])
                DMA(sg["kr_out"], krr[0:C, :], reads=["krr"])
            def step2B(si, sg):
                c0, C = sg['col0'], sg['C']
                TR("ps0", [(PS[0][:, 0:C], cqn[0:C, 0:128]), (PS[0][:, 128:128 + C], cqn[0:C, 128:256]),
                           (PS[0][:, 256:256 + C], ckvn[0:C, :]), (PS[0][0:32, 384:384 + C], krr[0:C, :])], ["cqn", "ckvn", "krr"])
                A("dve", lambda g, C=C, c0=c0: g.tensor_copy(cqnT[:, :, c0:c0 + C], PS[0][:, 0:256].rearrange("p (j e) -> p j e", j=2)[:, :, 0:C]), reads=["ps0"], writes=["cqnT"])
                A("act", lambda g, C=C, sg=sg: g.copy(sg["CKVTown"], PS[0][:, 256:256 + C]), reads=["ps0"], writes=[sg["ownkey"]])
                A("act", lambda g, C=C, sg=sg: g.copy(sg["KRTown"], PS[0][0:32, 384:384 + C]), reads=["ps0"], writes=[sg["ownkey"]])
                A("pool", lambda g, C=C, sg=sg: g.tensor_copy(sg["CKVown"], ckvn[0:C, :]), reads=["ckvn"], writes=[sg["ownkey"]])
            def step3():
                slot = [0]

                def fm(c0):
                    bank = (4, 5, 6, 7)[slot[0] % 4]
                    slot[0] += 1
                    key = f"ps{bank}"
                    MM(key, [(PS[bank][:, 0:n], W_in[:, k, c0:c0 + 128], hT[:, k, 0:n]) for k in range(8)], ["hT", "W_in"])
                    return PS[bank][:, 0:n], key
                W1 = nseg * (C0 + 2)
                for cc in range(2):
                    uxv = ux[:, cc, 0:W1].rearrange("p (s c) -> p s c", s=nseg)
                    for si, sg in enumerate(segs):
                        A("pool", lambda g, sg=sg, cc=cc, si=si, uxv=uxv: g.tensor_copy(uxv[:, si, 0:2], uxcar[:, sg["slot"], cc, :]), reads=["uxcar"], writes=["ux"])
                    p_x, kx = fm(cc * 128)
                    A("act", lambda g, p_x=p_x: g.copy(xin_sb[:, 0:n], p_x), reads=[kx], writes=["xin_sb"])
                    p_c, kc_ = fm(512 + cc * 128)
                    A("dve", lambda g, p_c=p_c, uxv=uxv: g.tensor_tensor(uxv[:, :, 2:2 + C0], segview(p_c), segview(xin_sb[:, 0:n]), ALU.mult), reads=[kc_, "xin_sb"], writes=["ux"])
                    cvv = segview(cv1[:, 0:n])
                    A("dve", lambda g, uxv=uxv, cvv=cvv, cc=cc: g.tensor_scalar(cvv, uxv[:, :, 0:C0], vec[:, 10 + cc * 3:11 + cc * 3], None, ALU.mult), reads=["ux", "vec"], writes=["cv1"])
                    A("dve", lambda g, uxv=uxv, cvv=cvv, cc=cc: g.scalar_tensor_tensor(cvv, uxv[:, :, 1:1 + C0], vec[:, 11 + cc * 3:12 + cc * 3], cvv, ALU.mult, ALU.add), reads=["ux", "vec", "cv1"], writes=["cv1"])
                    A("dve", lambda g, uxv=uxv, cvv=cvv, cc=cc: g.scalar_tensor_tensor(cvv, uxv[:, :, 2:2 + C0], vec[:, 12 + cc * 3:13 + cc * 3], cvv, ALU.mult, ALU.add), reads=["ux", "vec", "cv1"], writes=["cv1"])
                    p_g, kg = fm(768 + cc * 128)
                    A("act", lambda g, p_g=p_g: g.activation(sga[:, 0:n], p_g, AF.Silu), reads=[kg], writes=["sga"])
                    A("dve", lambda g: g.tensor_tensor(cv1[:, 0:n], cv1[:, 0:n], sga[:, 0:n], ALU.mult), reads=["cv1", "sga"], writes=["cv1"])
                    p_b, kb = fm(256 + cc * 128)
                    A("dve", lambda g, p_b=p_b, cc=cc: g.tensor_tensor(catT[:, cc, 0:n], p_b, cv1[:, 0:n], ALU.mult), reads=[kb, "cv1"], writes=["catT_a"])
                    for si, sg in enumerate(segs):
                        if sg["last"]:
                            DMA(sg["conv_out"][cc * 128:(cc + 1) * 128, :], uxv[:, si, C0:C0 + 2], reads=["ux"])
                        else:
                            A("pool", lambda g, sg=sg, cc=cc, si=si, uxv=uxv: g.tensor_copy(uxcar[:, sg["slot"], cc, :], uxv[:, si, C0:C0 + 2]), reads=["ux"], writes=["uxcar"])
                for j in range(4):
                    p_, k_ = fm(1440 + j * 128)
                    A("act", lambda g, p_=p_, j=j: g.activation(sgb[:, j, 0:n], p_, AF.Silu), reads=[k_], writes=["sgb"])
                W2 = nseg * (C0 + 1)
                zv4 = zcx[:, :, 0:W2].rearrange("p j (s c) -> p j s c", s=nseg)
                for si, sg in enumerate(segs):
                    A("pool", lambda g, sg=sg, si=si: g.tensor_copy(zv4[:, :, si, 0], zcar[:, sg["slot"], :]), reads=["zcar"], writes=["zcx"])
                if nseg == 1 and t.get("mid2b") is not None:
                    t["mid2b"]()
                for j in range(7):
                    p_, k_ = fm(1952 + j * 128)
                    zv = zcx[:, j, 0:W2].rearrange("p (s c) -> p s c", s=nseg)
                    if j % 2 == 0:
                        A("act", lambda g, p_=p_, zv=zv: g.copy(zv[:, :, 1:1 + C0], segview(p_)), reads=[k_], writes=["zcx"])
                    else:
                        A("dve", lambda g, p_=p_, zv=zv: g.tensor_copy(zv[:, :, 1:1 + C0], segview(p_)), reads=[k_], writes=["zcx"])
                zs4 = zs[:, :, 0:n].rearrange("p j (s c) -> p j s c", s=nseg)
                mub = vec[:, 16:23].unsqueeze(2).unsqueeze(3).broadcast_to([128, 7, nseg, C0])
                A("dve", lambda g: g.tensor_tensor(zs4, zv4[:, :, :, 0:C0], zv4[:, :, :, 1:1 + C0], ALU.subtract), reads=["zcx"], writes=["zs"])
                A("dve", lambda g: g.tensor_tensor(zs4, zs4, mub, ALU.mult), reads=["zs", "vec"], writes=["zs"])
                A("dve", lambda g: g.tensor_tensor(zs4, zs4, zv4[:, :, :, 1:1 + C0], ALU.add), reads=["zs", "zcx"], writes=["zs"])
                for si, sg in enumerate(segs):
                    A("pool", lambda g, sg=sg, si=si: g.tensor_copy(zcar[:, sg["slot"], :], zv4[:, :, si, C0]), reads=["zcx"], writes=["zcar"])
                for sg in segs:
                    if sg["last"]:
                        DMA(sg["shift_out"], zcar[:, sg["slot"], :], reads=["zcar"])
            if nseg == 1:
                step2A(0, segs[0])
                t["mid2b"] = lambda: step2B(0, segs[0])
                step3()
            else:
                for si, sg in enumerate(segs):
                    step2A(si, sg)
                    step2B(si, sg)
                step3()
            if t.get('next') is not None:
                front1a(t['next'])
            for h4 in range(2):
                for j in range(4):
                    h = h4 * 4 + j
                    MM("ps2", [(PS[2][0:64, j * 128:j * 128 + n], W_uq[:, k, h * 96:h * 96 + 64], cqnT[:, k, 0:n]) for k in range(2)], ["W_uq", "cqnT"])
                A("act", lambda g, h4=h4: g.copy(QnT[0:64, h4 * 4:h4 * 4 + 4, 0:n], PS[2][0:64, :].rearrange("p (j e) -> p j e", j=4)[:, :, 0:n]), reads=["ps2"], writes=["OaT"])
                for j in range(4):
                    h = h4 * 4 + j
                    MM("ps3", [(PS[3][0:32, j * 128:j * 128 + n], W_uq[:, k, h * 96 + 64:h * 96 + 96], cqnT[:, k, 0:n]) for k in range(2)], ["W_uq", "cqnT"])
                for j in range(4):
                    h = h4 * 4 + j
                    MM("ps0", [(PS[0][0:32, j * 128:j * 128 + n], W_uqsw[:, k, h * 32:h * 32 + 32], cqnT[:, k, 0:n]) for k in range(2)], ["W_uqsw", "cqnT"])
                cosb = cs_fm[:, 0, 0:n].unsqueeze(1).broadcast_to([32, 4, n])
                sinb = cs_fm[:, 1, 0:n].unsqueeze(1).broadcast_to([32, 4, n])
                A("dve", lambda g, cosb=cosb: g.tensor_tensor(qt1[:, :, 0:n], PS[3][0:32, :].rearrange("p (j e) -> p j e", j=4)[:, :, 0:n], cosb, ALU.mult), reads=["ps3", cfk], writes=[hfk])
                A("dve", lambda g, sinb=sinb: g.tensor_tensor(qt2[:, :, 0:n], PS[0][0:32, :].rearrange("p (j e) -> p j e", j=4)[:, :, 0:n], sinb, ALU.mult), reads=["ps0", cfk], writes=[hfk])
                A("dve", lambda g, h4=h4: g.tensor_tensor(QRT[:, h4 * 4:h4 * 4 + 4, 0:n], qt1[:, :, 0:n], qt2[:, :, 0:n], ALU.add), reads=[hfk], writes=["QRT"])
                for j in range(4):
                    h = h4 * 4 + j
                    MM("ps1", [(PS[1][:, j * 128:j * 128 + n], W_ukT[0:64, h, :], QnT[0:64, h, 0:n])], ["W_ukT", "OaT"])
                A("dve", lambda g, h4=h4: g.tensor_copy(QaT[:, h4 * 4:h4 * 4 + 4, 0:n], PS[1][:].rearrange("p (j e) -> p j e", j=4)[:, :, 0:n]), reads=["ps1"], writes=["QaT"])
            if t.get('next') is not None:
                front1b(t['next'])
                t['next']['front_done'] = True
            OB = [(6, 0), (6, 1), (6, 2), (7, 0), (7, 1), (7, 2), (2, 0), (2, 1)]
            SB = [3, 4, 5]

            def gen_attn():
                for sg in segs:
                    c0, C = sg["col0"], sg["C"]
                    if sg.get("preattn") is not None:
                        sg["preattn"]()
                        yield
                    kts = sg["ktiles"]
                    units = [(ki, hf_) for ki in range(len(kts)) for hf_ in range(2)]

                    def QK(u):
                        ki, hf_ = units[u]
                        ckvt_ap, krt_ap, ckv_ap, kn, diag, kkey = kts[ki]
                        bk = SB[u % 3]
                        o = PS[bk][0:kn, 0:4 * C].rearrange("p (h c) -> p h c", h=4)
                        MM(f"ps{bk}", [(o, ckvt_ap, QaT[:, 4 * hf_:4 * hf_ + 4, c0:c0 + C]),
                                       (o, krt_ap, QRT[0:32, 4 * hf_:4 * hf_ + 4, c0:c0 + C])], [kkey, "QaT", "QRT"])
                    def onorm(bi, ob, h0, nh, C=C, c0=c0):
                        tb = ob
                        ov = PS[ob][0:C, 0:nh * 132].rearrange("p (s e) -> p s e", e=132)
                        ot = otok3[0:C, bi % 2, 0:nh, :]
                        A("dve", lambda g: g.reciprocal(rden[0:C, bi % 2, 0:nh].unsqueeze(2), ov[:, :, 128:129]), reads=[f"ps{ob}"], writes=[f"rden{bi % 2}"])
                        A("dve", lambda g: g.tensor_tensor(ot, ov[:, :, 0:128], rden[0:C, bi % 2, 0:nh].unsqueeze(2).broadcast_to([C, nh, 128]), ALU.mult), reads=[f"ps{ob}", f"rden{bi % 2}"], writes=[f"otok{bi % 2}"])
                        TR(f"ps{tb}", [(PSB(tb)[:, j * 128:j * 128 + C], otok3[0:C, bi % 2, j, :]) for j in range(nh)], [f"otok{bi % 2}"], bf=True)
                        A("act", lambda g: g.copy(OaT[:, h0:h0 + nh, c0:c0 + C], PSB(tb)[:, 0:nh * 128].rearrange("p (j e) -> p j e", e=128)[:, :, 0:C]), reads=[f"ps{tb}"], writes=["OaT"])
                    QK(0)
                    started = set()
                    for u, (ki, hf_) in enumerate(units):
                        ckvt_ap, krt_ap, ckv_ap, kn, diag, kkey = kts[ki]
                        bk = SB[u % 3]
                        pb = u % 2
                        if u + 1 < len(units):
                            QK(u + 1)
                        A("act", lambda g, pb=pb, kn=kn, C=C, bk=bk: g.activation(PT[0:kn, pb, 0:4 * C], PS[bk][0:kn, 0:4 * C], AF.Exp, scale=SCALE), reads=[f"ps{bk}"], writes=[f"PT{pb}"])
                        if diag and C == 128:
                            A("dve", lambda g, pb=pb: g.tensor_tensor(PT[:, pb, :].rearrange("p (h c) -> p h c", h=4), PT[:, pb, :].rearrange("p (h c) -> p h c", h=4),
                                                                       amask[:].unsqueeze(1).broadcast_to([128, 4, 128]), ALU.mult), reads=[f"PT{pb}", "amask"], writes=[f"PT{pb}"])
                        last = (ki == len(kts) - 1)
                        for j in range(4):
                            h = 4 * hf_ + j
                            ob, sl = OB[h]
                            first = ob not in started
                            started.add(ob)
                            A("pe", lambda g, ckv_ap=ckv_ap, kn=kn, pb=pb, C=C, ob=ob, sl=sl, j=j, first=first, last=last:
                              g.matmul(PS[ob][0:C, sl * 132:sl * 132 + 129], lhsT=PT[0:kn, pb, j * C:(j + 1) * C], rhs=ckv_ap, start=first, stop=last, skip_group_check=True),
                              reads=[kkey, f"PT{pb}"], writes=[f"ps{ob}"])
                        if last and hf_ == 0:
                            onorm(0, 6, 0, 3)
                        yield
                    for bi, (ob, h0, nh) in ((1, (7, 3, 3)), (2, (2, 6, 2))):
                        onorm(bi, ob, h0, nh)
                        yield
                for hp in range(4):
                    tb = 4 + hp % 2
                    MM(f"ps{tb}", [(PS[tb][:, 0:n], W_uvp[:, 2 * hp, :], OaT[:, 2 * hp, 0:n]), (PS[tb][:, 0:n], W_uvp[:, 2 * hp + 1, :], OaT[:, 2 * hp + 1, 0:n])], ["W_uvp", "OaT"])
                    A("dve", lambda g, hp=hp, tb=tb: g.tensor_tensor(catT[:, 2 + hp, 0:n], PS[tb][:, 0:n], sgb[:, hp, 0:n], ALU.mult), reads=[f"ps{tb}", "sgb"], writes=["catT_b"])
                    yield

            def gen_rwkv():
                A("act", lambda g: g.activation(th[:, 0:n], zs[0:64, 6, 0:n], AF.Tanh), reads=["zs"], writes=["th"])
                for c in range(2):
                    MM("ps0", [(PS[0][:, c * 128:c * 128 + n], dwa[0:64, c * 128:(c + 1) * 128], th[0:64, 0:n])], ["dwa", "th"])
                    A("act", lambda g, c=c: g.activation(lw[:, c, 0:n], PS[0][:, c * 128:c * 128 + n], AF.Sigmoid, bias=vec[:, 23 + c:24 + c]), reads=["ps0", "vec"], writes=["lw"])
                for c in range(2):
                    PE1(PS[1][:, c * 128:c * 128 + n], dwa[64:128, c * 128:(c + 1) * 128], zs[64:128, 6, 0:n], True, True, ["dwa", "zs"], ["ps1"], r32=False, rb=64)
                    A("act", lambda g, c=c: g.activation(av[:, c, 0:n], PS[1][:, c * 128:c * 128 + n], AF.Sigmoid, bias=vec[:, 25 + c:26 + c]), reads=["ps1", "vec"], writes=["av"])
                yield
                A("dve", lambda g: g.tensor_scalar(lw[:, :, 0:n], lw[:, :, 0:n], LOGW_C, None, ALU.mult), reads=["lw"], writes=["lw"])
                sm = scanm[:, 0:128] if nseg == 1 else scanm[:, 128:256]
                for c in range(2):
                    A("dve", lambda g, c=c, sm=sm: g.tensor_tensor_scan(G[:, c, 0:n], sm[:, 0:n], lw[:, c, 0:n], 0.0, ALU.mult, ALU.add), reads=["scanm", "lw"], writes=["G"])
                A("act", lambda g: g.activation(Epos[:, :, 0:n], G[:, :, 0:n], AF.Exp), reads=["G"], writes=["Epos"])
                A("act", lambda g: g.activation(Eneg[:, :, 0:n], G[:, :, 0:n], AF.Exp, scale=-1.0), reads=["G"], writes=["Eneg"])
                A("dve", lambda g: g.tensor_tensor(Eprev[:, :, 0:n], G[:, :, 0:n], lw[:, :, 0:n], ALU.subtract), reads=["G", "lw"], writes=["Eprev"])
                A("act", lambda g: g.activation(Eprev[:, :, 0:n], Eprev[:, :, 0:n], AF.Exp), reads=["Eprev"], writes=["Eprev"])
                yield
                A("dve", lambda g: g.tensor_tensor(kk[:, :, 0:n], zs[:, 2:4, 0:n], vec[:, 27:29].unsqueeze(2).broadcast_to([128, 2, n]), ALU.mult), reads=["zs", "vec"], writes=["kk"])
                A("dve", lambda g: g.tensor_tensor(bb[:, :, 0:n], kk[:, :, 0:n], kk[:, :, 0:n], ALU.mult), reads=["kk"], writes=["bb"])
                for c in range(2):
                    MM("ps0", [(PS[0][:, 256 + c * 128:256 + c * 128 + n], blk[:], bb[:, c, 0:n])], ["blk", "bb"])
                n2v = PS[0][:, 256:512].rearrange("p (c e) -> p c e", c=2)[:, :, 0:n]
                A("dve", lambda g: g.tensor_scalar(bb[:, :, 0:n], n2v, 1e-24, None, ALU.max), reads=["ps0"], writes=["bb"])
                A("act", lambda g: g.activation(bb[:, :, 0:n], bb[:, :, 0:n], AF.Ln), reads=["bb"], writes=["bb"])
                A("act", lambda g: g.activation(bb[:, :, 0:n], bb[:, :, 0:n], AF.Exp, scale=-0.5), reads=["bb"], writes=["bb"])
                A("dve", lambda g: g.tensor_tensor(kk[:, :, 0:n], kk[:, :, 0:n], bb[:, :, 0:n], ALU.mult), reads=["kk", "bb"], writes=["kk"])
                yield
                A("dve", lambda g: g.tensor_tensor(kmod[:, :, 0:n], av[:, :, 0:n], vec[:, 29:31].unsqueeze(2).broadcast_to([128, 2, n]), ALU.mult), reads=["av", "vec"], writes=["kmod"])
                A("dve", lambda g: g.tensor_tensor(kmod[:, :, 0:n], kmod[:, :, 0:n], omka[:, 0:2].unsqueeze(2).broadcast_to([128, 2, n]), ALU.add), reads=["kmod", "omka"], writes=["kmod"])
                A("dve", lambda g: g.tensor_tensor(kmod[:, :, 0:n], kmod[:, :, 0:n], zs[:, 2:4, 0:n], ALU.mult), reads=["kmod", "zs"], writes=["kmod"])
                A("dve", lambda g: g.tensor_tensor(bb[:, :, 0:n], kk[:, :, 0:n], av[:, :, 0:n], ALU.mult), reads=["kk", "av", "bb"], writes=["bb"])
                for c in range(2):
                    arv = AR[:, c, 0:2 * n].rearrange("p (s t e) -> p s t e", s=nseg, t=2)
                    A("dve", lambda g, c=c, arv=arv: g.scalar_tensor_tensor(arv[:, :, 0, :], segview(kk[:, c, 0:n]), -1.0, segview(Eprev[:, c, 0:n]), ALU.mult, ALU.mult), reads=["kk", "Eprev"], writes=["AR"])
                    A("dve", lambda g, c=c, arv=arv: g.tensor_tensor(arv[:, :, 1, :], segview(zs[:, c, 0:n]), segview(Epos[:, c, 0:n]), ALU.mult), reads=["zs", "Epos"], writes=["AR"])
                yield
                A("dve", lambda g: g.tensor_tensor(BT[:, :, 0:n], bb[:, :, 0:n], Eneg[:, :, 0:n], ALU.mult), reads=["bb", "Eneg"], writes=["BT"])
                A("dve", lambda g: g.tensor_tensor(KT[:, :, 0:n], kmod[:, :, 0:n], Eneg[:, :, 0:n], ALU.mult), reads=["kmod", "Eneg"], writes=["KT"])
                A("act", lambda g: g.copy(BTb[:, :, 0:n], BT[:, :, 0:n]), reads=["BT"], writes=["BTb"])
                A("act", lambda g: g.copy(KTb[:, :, 0:n], KT[:, :, 0:n]), reads=["KT"], writes=["KTb"])
                A("pool", lambda g: g.tensor_tensor(rkp[:, :, 0:n], zs[:, 0:2, 0:n], kmod[:, :, 0:n], ALU.mult), reads=["zs", "kmod", "G"], writes=["G"])
                for si, sg in enumerate(segs):
                    c0, C, slot_ = sg["col0"], sg["C"], sg["slot"]
                    nlev = {128: 6, 64: 5, 16: 3}[C]
                    mo = {128: 0, 64: 256, 16: 384}[C]
                    MM("ps0", [(PS[0][0:C, 0:4], rkp[:, c, c0:c0 + C], bonmat[:, c, :]) for c in range(2)], ["G", "bonmat"])
                    A("act", lambda g, C=C, si=si: g.copy(rkb[0:C, si, :], PS[0][0:C, 0:4]), reads=["ps0"], writes=["rkb"])
                    TR("ps1", [(PSB(1)[0:C, c * 128:(c + 1) * 128], BTb[:, c, c0:c0 + C]) for c in range(2)], ["BTb"], bf=True)
                    A("dve", lambda g, C=C, si=si: g.tensor_copy(Btok[0:C, si, :], PSB(1)[0:C, 0:256]), reads=["ps1"], writes=["Btok"])
                    yield
                    TR("ps0", [(PSB(0)[0:C, c * 128:(c + 1) * 128], KTb[:, c, c0:c0 + C]) for c in range(2)], ["KTb"], bf=True)
                    A("act", lambda g, C=C, si=si: g.copy(Ktok[0:C, si, :], PSB(0)[0:C, 0:256]), reads=["ps0"], writes=["Ktok"])
                    TR("ps1", [(PS[1][0:C, c * 128:(c + 1) * 128], zs[:, 4 + c, c0:c0 + C]) for c in range(2)], ["zs"])
                    A("dve", lambda g, C=C, si=si: g.tensor_copy(Vtok[0:C, si, :], PS[1][0:C, 0:256]), reads=["ps1"], writes=["Vtok"])
                    A("act", lambda g, C=C, si=si: g.copy(Vtokb[0:C, si, :], PS[1][0:C, 0:256]), reads=["ps1"], writes=["Vtokb"])
                    yield
                    for c in range(2):
                        def hrows(hh):
                            return slice(64 * hh, 64 * hh + 64)

                        for hh in range(2):
                            PE1(PS[hh][0:C, 0:2 * C], BTb[hrows(hh), c, c0:c0 + C], AR[hrows(hh), c, 2 * c0:2 * c0 + 2 * C], True, True, ["BTb", "AR"], [f"ps{hh}"], rb=64 * hh)
                        for hh in range(2):
                            PE1(PS[hh][0:C, 256:256 + 2 * C], KTb[hrows(hh), c, c0:c0 + C], AR[hrows(hh), c, 2 * c0:2 * c0 + 2 * C], True, True, ["KTb", "AR"], [f"ps{hh}"], rb=64 * hh)
                        mgb = mg[0:C, mo:mo + 2 * C].unsqueeze(1).broadcast_to([C, 2, 2 * C])
                        for hh in range(2):
                            A("dve", lambda g, C=C, mgb=mgb, hh=hh: g.tensor_tensor(Gm[0:C, hh, :].rearrange("p (t e) -> p t e", t=2)[:, :, 0:2 * C], PS[hh][0:C, :].rearrange("p (t e) -> p t e", t=2)[:, :, 0:2 * C], mgb, ALU.mult), reads=[f"ps{hh}", "mg"], writes=["Gm"])
                        yield
                        for hh in range(2):
                            PE1(PS[hh][0:C, 0:C], AR[hrows(hh), c, 2 * c0:2 * c0 + C], BTb[hrows(hh), c, c0:c0 + C], True, True, ["BTb", "AR"], [f"ps{hh}"], rb=64 * hh)
                        for hh in range(2):
                            A(("dve", "act")[hh] if False else "dve", lambda g, C=C, hh=hh: g.tensor_tensor(NXT[0:C, 0, hh, 0:C], PS[hh][0:C, 0:C], msl[0:C, 0:C], ALU.mult), reads=[f"ps{hh}", "msl"], writes=["NXT0"])
                        A("act", lambda g, C=C: g.copy(NX[0:C, 0, :, 0:C], Gm[0:C, :, 0:C]), reads=["Gm"], writes=["NX0"])
                        A("dve", lambda g, C=C: g.tensor_tensor(NW[0:C, 0, :, 0:C], Gm[0:C, :, 0:C], ident[0:C, 0:C].unsqueeze(1).broadcast_to([C, 2, C]), ALU.add), reads=["Gm", "ident"], writes=["NW0"])
                        for lev in range(nlev):
                            yield
                            a, b_ = lev % 2, (lev + 1) % 2
                            lastlev = (lev == nlev - 1)

                            def fnN(g, ps, o, L, Rr, C=C):
                                ins = None
                                for hh in range(2):
                                    ins = g.matmul(ps[0:C, o + hh * 128:o + hh * 128 + C], lhsT=L[:, hh, :], rhs=Rr[:, hh, :], start=True, stop=True)
                                return ins
                            A("pe", lambda g, fnN=fnN, a=a, C=C: fnN(g, PS[0], 0, NX[0:C, a, :, 0:C], NXT[0:C, a, :, 0:C]), reads=[f"NX{a}", f"NXT{a}"], writes=["ps0"])
                            A("act", lambda g, C=C, b_=b_: g.copy(NXT[0:C, b_, :, 0:C], PS[0][0:C, 0:256].rearrange("p (h e) -> p h e", h=2)[:, :, 0:C]), reads=["ps0"], writes=[f"NXT{b_}"])
                            if not lastlev:
                                A("pe", lambda g, fnN=fnN, a=a, C=C: fnN(g, PS[1], 0, NXT[0:C, a, :, 0:C], NX[0:C, a, :, 0:C]), reads=[f"NX{a}", f"NXT{a}"], writes=["ps1"])
                                A("dve", lambda g, C=C, b_=b_: g.tensor_copy(NX[0:C, b_, :, 0:C], PS[1][0:C, 0:256].rearrange("p (h e) -> p h e", h=2)[:, :, 0:C]), reads=["ps1"], writes=[f"NX{b_}"])
                            A("pe", lambda g, fnN=fnN, a=a, b_=b_, C=C: fnN(g, PS[0], 256, NXT[0:C, b_, :, 0:C], NW[0:C, a, :, 0:C]), reads=[f"NXT{b_}", f"NW{a}"], writes=["ps0"])
                            A("dve", lambda g, C=C, a=a, b_=b_, lastlev=lastlev: g.tensor_tensor(NW[0:C, b_, :, 0:C], PS[0][0:C, 256:512].rearrange("p (h e) -> p h e", h=2)[:, :, 0:C], NW[0:C, a, :, 0:C], ALU.add), reads=["ps0", f"NW{a}"], writes=[f"NW{b_}"])
                        yield
                        wf = nlev % 2
                        Hc = Hp[:, slot_, c, :]
                        hk = f"Hp{slot_}_{c}"

                        A("act", lambda g, Hc=Hc: g.copy(Hb[:], Hc), reads=[hk], writes=["Hb"])
                        for hh in range(2):
                            PE1(PS[hh][0:C, 256:320], AR[hrows(hh), c, 2 * c0:2 * c0 + C], Hb[hrows(hh), hh * 64:hh * 64 + 64], True, False, ["AR", "Hb"], [f"ps{hh}"], rb=64 * hh)
                        for hh in range(2):
                            h = 2 * c + hh
                            PE1(PS[hh][0:C, 256:320], Gm[0:C, hh, 256:256 + C], Vtokb[0:C, si, h * 64:h * 64 + 64], False, True, ["Gm", "Vtokb"], [f"ps{hh}"], rb=0)
                        A("act", lambda g, C=C: g.copy(X1[0:C, 0:64], PS[0][0:C, 256:320]), reads=["ps0"], writes=["X1"])
                        A("dve", lambda g, C=C: g.tensor_copy(X1[0:C, 64:128], PS[1][0:C, 256:320]), reads=["ps1"], writes=["X1"])

                        def fnU(g, C=C, wf=wf):
                            ins = None
                            for hh in range(2):
                                ins = g.matmul(PS[1][0:C, 384 + hh * 64:384 + hh * 64 + 64], lhsT=NW[0:C, wf, hh, 0:C], rhs=X1[0:C, hh * 64:hh * 64 + 64], start=True, stop=True)
                            return ins
                        A("pe", fnU, reads=[f"NW{wf}", "X1"], writes=["ps1"])
                        A("dve", lambda g, C=C: g.tensor_copy(U[0:C, :], PS[1][0:C, 384:512]), reads=["ps1"], writes=["U"])

                        for hh in (1, 0):
                            PE1(PS[hh][0:C, 320:384], AR[hrows(hh), c, 2 * c0 + C:2 * c0 + 2 * C], Hb[hrows(hh), hh * 64:hh * 64 + 64], True, False, ["AR", "Hb"], [f"ps{hh}"], rb=64 * hh)
                        for hh in (0, 1):
                            h = 2 * c + hh
                            PE1(PS[hh][0:C, 320:384], Gm[0:C, hh, C:2 * C], U[0:C, hh * 64:hh * 64 + 64], False, False, ["Gm", "U"], [f"ps{hh}"], rb=0)
                            PE1(PS[hh][0:C, 320:384], Gm[0:C, hh, 256 + C:256 + 2 * C], Vtokb[0:C, si, h * 64:h * 64 + 64], False, True, ["Gm", "Vtokb"], [f"ps{hh}"], rb=0)
                        A("act", lambda g, C=C, c=c: g.copy(yraw[0:C, c * 128:c * 128 + 64], PS[0][0:C, 320:384]), reads=["ps0"], writes=["yraw"])
                        A("dve", lambda g, C=C, c=c: g.tensor_copy(yraw[0:C, c * 128 + 64:c * 128 + 128], PS[1][0:C, 320:384]), reads=["ps1"], writes=["yraw"])
                        MM("ps0", [(PS[0][:, 0:128], R(identr[:]), R(Hc)), (PS[0][:, 0:128], Btok[0:C, si, c * 128:(c + 1) * 128], U[0:C, :]),
                                   (PS[0][:, 0:128], Ktok[0:C, si, c * 128:(c + 1) * 128], Vtokb[0:C, si, c * 128:(c + 1) * 128])], ["identr", hk, "Btok", "Ktok", "Vtokb", "U"])
                        A("dve", lambda g, Hc=Hc, c=c, c0=c0, C=C: g.scalar_tensor_tensor(R(Hc), PS[0][:, 0:128], Epos[:, c, c0 + C - 1:c0 + C], blk[:], ALU.mult, ALU.mult), reads=["ps0", "Epos", "blk"], writes=[hk])
                        if sg["last"]:
                            TR("ps1", [(PS[1][:, 0:128], Hc)], [hk])
                            A("act", lambda g: g.copy(wko[:], PS[1][:, 0:128]), reads=["ps1"], writes=["wko"])
                            for hh in range(2):
                                h = 2 * c + hh
                                DMA(sg["wkv_out"][h * 64:(h + 1) * 64, :], wko[hrows(hh), hh * 64:hh * 64 + 64], reads=["wko"])
                    yield
                    Yv = yraw[0:C, :].rearrange("p (h e) -> p h e", h=4)
                    ykeys = ["yraw"]
                    A("dve", lambda g, C=C, Yv=Yv: g.tensor_reduce(gst[0:C, 0:4], Yv, AX.X, ALU.add), reads=ykeys, writes=["gst"])
                    A("dve", lambda g, C=C: g.tensor_scalar(gst[0:C, 0:4], gst[0:C, 0:4], 1.0 / 64, None, ALU.mult), reads=["gst"], writes=["gst"])
                    ycv = yc[0:C, :].rearrange("p (h e) -> p h e", h=4)
                    A("dve", lambda g, C=C, Yv=Yv, ycv=ycv: g.tensor_tensor(ycv, Yv, gst[0:C, 0:4].unsqueeze(2).broadcast_to([C, 4, 64]), ALU.subtract), reads=ykeys + ["gst"], writes=["yc"])
                    A("dve", lambda g, C=C: g.tensor_tensor(ysq[0:C, :], yc[0:C, :], yc[0:C, :], ALU.mult), reads=["yc"], writes=["ybv"])
                    A("dve", lambda g, C=C: g.tensor_reduce(gst[0:C, 4:8], ysq[0:C, :].rearrange("p (h e) -> p h e", h=4), AX.X, ALU.add), reads=["ybv"], writes=["gst"])
                    A("act", lambda g, C=C: g.activation(gst[0:C, 4:8], gst[0:C, 4:8], AF.Ln, bias=epsb[0:C, 1:2], scale=1.0 / 64), reads=["gst", "epsb"], writes=["gst"])
                    A("act", lambda g, C=C: g.activation(gst[0:C, 4:8], gst[0:C, 4:8], AF.Exp, scale=-0.5), reads=["gst"], writes=["gst"])
                    A("dve", lambda g, C=C, ycv=ycv: g.tensor_tensor(ycv, ycv, gst[0:C, 4:8].unsqueeze(2).broadcast_to([C, 4, 64]), ALU.mult), reads=["yc", "gst"], writes=["yc"])
                    A("dve", lambda g, C=C: g.tensor_tensor(yc[0:C, :], yc[0:C, :], lnw_bc[0:C, :], ALU.mult), reads=["yc", "lnw_bc"], writes=["yc"])
                    A("dve", lambda g, C=C: g.tensor_tensor(yc[0:C, :], yc[0:C, :], lnb_bc[0:C, :], ALU.add), reads=["yc", "lnb_bc"], writes=["yc"])
                    A("dve", lambda g, C=C, si=si: g.tensor_tensor(ybv[0:C, :].rearrange("p (h e) -> p h e", h=4), Vtok[0:C, si, :].rearrange("p (h e) -> p h e", h=4), rkb[0:C, si, :].unsqueeze(2).broadcast_to([C, 4, 64]), ALU.mult), reads=["Vtok", "rkb", "gst"], writes=["ybv"])
                    A("dve", lambda g, C=C: g.tensor_tensor(yc[0:C, :], yc[0:C, :], ybv[0:C, :], ALU.add), reads=["yc", "ybv"], writes=["yc"])
                    A("dve", lambda g, C=C, si=si: g.tensor_tensor(ycb[0:C, :], yc[0:C, :], sgc[0:C, si, :], ALU.mult), reads=["yc", "sgc"], writes=["ycb"])
                    TR("ps1", [(PSB(1)[:, cc * 128:cc * 128 + C], ycb[0:C, cc * 128:(cc + 1) * 128]) for cc in range(2)], ["ycb"], bf=True)
                    A("dve", lambda g, C=C, c0=c0: g.tensor_copy(catT[:, 6:8, c0:c0 + C], PSB(1)[:, 0:256].rearrange("p (j e) -> p j e", j=2)[:, :, 0:C]), reads=["ps1"], writes=["catT_c"])

            gens = [gen_attn(), gen_rwkv()]
            weights = [1, 2]
            while gens:
                done = []
                for g_, w_ in zip(gens, weights):
                    for _ in range(w_):
                        try:
                            next(g_)
                        except StopIteration:
                            done.append(g_)
                            break
                for g_ in done:
                    ix = gens.index(g_)
                    gens.pop(ix)
                    weights.pop(ix)
            for half in range(2):
                MM(f"ps{2 + half}", [(PS[2 + half][0:n, :], catT[:, f, 0:n], W_out[:, f, half * 512:(half + 1) * 512]) for f in range(8)], ["catT_a", "catT_b", "catT_c", "W_out"])
                A("dve", lambda g, half=half: g.tensor_tensor(X[0:n, half * 512:(half + 1) * 512], X[0:n, half * 512:(half + 1) * 512], PS[2 + half][0:n, :], ALU.add), reads=[f"ps{2 + half}", xk], writes=[xk])
            if x_dst is not None:
                DMA(x_dst, X[0:n, :], reads=[xk], writes=["scr"])
            if y_dst is not None:
                A("dve", lambda g: g.memset(ss[:, 3:4], 0.0), writes=["ss"])
                A("act", lambda g: g.activation(hf[0:n, :], X[0:n, :], AF.Square, accum_out=ss[0:n, 3:4]), reads=[xk, "ss"], writes=[hfk, "ss"])
                rstd_from_ss(n, 3, D, RMS_EPS)
                A("dve", lambda g: g.scalar_tensor_tensor(hf[0:n, :], X[0:n, :], rs_[0:n, 3:4], fg_bc[0:n, :], ALU.mult, ALU.mult), reads=[xk, "rs_", "fg_bc"], writes=[hfk])
                DMA(y_dst, hf[0:n, :], reads=[hfk])

        tiles = []
        for l in range(2):
            for ti in range(NPT):
                tok0 = 0 if ti == 0 else 16 + (ti - 1) * 128
                n = 16 if ti == 0 else 128
                kts = []
                if ti >= 1:
                    kts.append((CKVT_p[:, 0:16], KRT_p[0:32, 0:16], CKV_p[0:16, 0, :], 16, False, "kvp"))
                for kj in range(1, ti):
                    k0 = 16 + (kj - 1) * 128
                    kts.append((CKVT_p[:, k0:k0 + 128], KRT_p[0:32, k0:k0 + 128], CKV_p[:, kj, :], 128, False, "kvp"))
                kts.append((CKVT_p[:, tok0:tok0 + n], KRT_p[0:32, tok0:tok0 + n], CKV_p[0:n, ti, :], n, True, "kvp"))
                sg = dict(slot=0, col0=0, C=n, pos0=tok0, ckv_out=ckv_p[l, tok0:tok0 + n, :], kr_out=kr_p[l, tok0:tok0 + n, :],
                          ktiles=kts, last=(ti == NPT - 1), CKVTown=CKVT_p[:, tok0:tok0 + n], KRTown=KRT_p[0:32, tok0:tok0 + n],
                          CKVown=CKV_p[0:n, ti, 0:128], ownkey="kvp", conv_out=conv_p[l], shift_out=shift_p[l], wkv_out=wkv_p[l], preattn=None)
                tiles.append(dict(l=l, n=n, segs=[sg], first=(ti == 0), kind="p",
                                  x_src=(xp[tok0:tok0 + n, :] if l == 0 else scr_p[tok0:tok0 + n, :]),
                                  x_dst=(scr_p[tok0:tok0 + n, :] if l == 0 else None),
                                  y_dst=(y_p[tok0 - 16:tok0 - 16 + n, :] if (l == 1 and ti >= 1) else None)))
            for tj in range(NS // 2):
                segs = []
                for si in range(2):
                    b = 2 * tj + si
                    slot_ = 1 + si

                    def preattn(l=l, b=b):
                        for kt in range(PAST // 128):
                            bs = stage_load(cckv[l, b, kt * 128:(kt + 1) * 128, :], 128)
                            DMA(stage[:, bs, 128:160], ckr[l, b, kt * 128:(kt + 1) * 128, :], writes=[f"stage{bs}"])
                            A("pool", lambda g, bs=bs, kt=kt: g.tensor_copy(CKV_s[:, kt, 0:128], stage[:, bs, 0:128]), reads=[f"stage{bs}"], writes=["kvs"])
                            TR("ps0", [(PS[0][:, 0:128], stage[:, bs, 0:128]), (PS[0][0:32, 128:256], stage[:, bs, 128:160])], [f"stage{bs}"])
                            A("act", lambda g, kt=kt: g.copy(CKVT_s[:, kt * 128:(kt + 1) * 128], PS[0][:, 0:128]), reads=["ps0"], writes=["kvs"])
                            A("dve", lambda g, kt=kt: g.tensor_copy(KRT_s[0:32, kt * 128:(kt + 1) * 128], PS[0][0:32, 128:256]), reads=["ps0"], writes=["kvs"])

                    def prestate(l=l, b=b, slot_=slot_):
                        for cc in range(2):
                            DMA(uxcar[:, slot_, cc, :], sconvT[l, b, cc * 128:(cc + 1) * 128, :], writes=["uxcar"])
                        DMA(zcar[:, slot_, :], sshiftT[l, b], writes=["zcar"])
                        for c in range(2):
                            A("pool", lambda g: g.memset(Sp[:], 0.0), writes=["Sp"])
                            for hh in range(2):
                                h = 2 * c + hh
                                DMA(Sp[64 * hh:64 * hh + 64, 64 * hh:64 * hh + 64], swkv[l, b, h * 64:(h + 1) * 64, :], writes=["Sp"])
                            TR("ps0", [(PS[0][:, 0:128], Sp[:])], ["Sp"])
                            A("dve", lambda g, slot_=slot_, c=c: g.tensor_copy(R(Hp[:, slot_, c, :]), PS[0][:, 0:128]), reads=["ps0"], writes=[f"Hp{slot_}_{c}"])
                    kts = []
                    for kt in range(PAST // 128):
                        kts.append((CKVT_s[:, kt * 128:(kt + 1) * 128], KRT_s[0:32, kt * 128:(kt + 1) * 128], CKV_s[:, kt, :], 128, False, "kvs"))
                    kts.append((CKVT_o[:, si, :], KRT_o[0:32, si, :], CKV_o[0:64, si, :], 64, True, f"kvo{si}"))
                    segs.append(dict(slot=slot_, col0=64 * si, C=64, pos0=TP, ckv_out=ckv_s[l, b], kr_out=kr_s[l, b], ktiles=kts, last=True,
                                     CKVTown=CKVT_o[:, si, :], KRTown=KRT_o[0:32, si, :], CKVown=CKV_o[0:64, si, 0:128], ownkey=f"kvo{si}",
                                     conv_out=conv_s[l, b], shift_out=shift_s[l, b], wkv_out=wkv_s[l, b], preattn=preattn, prestate=prestate))
                r0 = tj * 128
                tiles.append(dict(l=l, n=128, segs=segs, first=False, kind="s",
                                  x_src=(xs[r0:r0 + 128, :] if l == 0 else scr_s[r0:r0 + 128, :]),
                                  x_dst=(scr_s[r0:r0 + 128, :] if l == 0 else None),
                                  y_dst=(y_s[r0:r0 + 128, :] if l == 1 else None)))
        for i, t in enumerate(tiles):
            t["par"] = i % 2
            t["next"] = tiles[i + 1] if (i + 1 < len(tiles) and tiles[i + 1]["l"] == t["l"]) else None
        A("pool", lambda g: g.memset(epsb[:, 0:1], RMS_EPS), writes=["epsb"])
        A("pool", lambda g: g.memset(epsb[:, 1:2], GN_EPS), writes=["epsb"])
        cur_l = -1
        for i, t in enumerate(tiles):
            if t["l"] != cur_l:
                cur_l = t["l"]
                load_weights(cur_l)
                load_x(t)
                load_rope(t)
            if t["kind"] == "p" and t["first"]:
                A("pool", lambda g: g.memset(uxcar[:, 0, :, :], 0.0), writes=["uxcar"])
                A("pool", lambda g: g.memset(zcar[:, 0, :], 0.0), writes=["zcar"])
                for c in range(2):
                    A("dve", lambda g, c=c: g.tensor_scalar(R(Hp[:, 0, c, :]), blk[:], 0.0, None, ALU.mult), reads=["blk"], writes=[f"Hp0_{c}"])
            if t["kind"] == "s":
                for sg in t["segs"]:
                    sg["prestate"]()
            if i + 1 < len(tiles) and tiles[i + 1]["l"] == t["l"]:
                load_x(tiles[i + 1])
                load_rope(tiles[i + 1])
            do_tile(t)
        S.emit()
    return nc


def host_consts(TP, PAST):
    ident = np.eye(128, dtype=np.float32)
    blk = np.zeros((128, 128), np.float32)
    blk[:64, :64] = 1
    blk[64:, 64:] = 1
    mg = np.zeros((128, 416), np.float32)
    for C, mo in ((128, 0), (64, 256), (16, 384)):
        s = np.arange(C)[:, None]
        t = np.arange(C)[None, :]
        mg[:C, mo:mo + C] = (s < t)
        mg[:C, mo + C:mo + 2 * C] = (s <= t)
    tt = np.arange(128)[:, None]
    s_ = np.arange(128)[None, :]
    sl = (s_ < tt).astype(np.float32)
    k = np.arange(128)[:, None]
    q = np.arange(128)[None, :]
    am = np.ones((128, 128), np.float32)
    am[(k >= 64) & (q < 64)] = 0
    scan = np.ones((128, 256), np.float32)
    scan[:, 0] = 0
    scan[:, 128] = 0
    scan[:, 128 + 64] = 0
    pos = np.concatenate([np.arange(TP), PAST + np.arange(64)]).astype(np.float32)
    inv = (10000.0 ** (-np.arange(16, dtype=np.float32) * 2.0 / 32)).astype(np.float32)
    ang = (pos[:, None] * inv[None, :]).astype(np.float32)
    cos = np.cos(ang).astype(np.float32)
    sin = np.sin(ang).astype(np.float32)
    rope_tm = np.concatenate([cos, sin], axis=1).astype(np.float32)
    rope_fm = np.stack([np.concatenate([cos, cos], 1).T, np.concatenate([sin, sin], 1).T], axis=1)
    return dict(c_ident=ident, c_blk=blk, c_mg=mg, c_sl=sl, c_am=am, c_scan=scan,
                rope_tm=np.ascontiguousarray(rope_tm), rope_fm=np.ascontiguousarray(rope_fm.astype(np.float32)))


def col128(v):
    v = np.asarray(v, np.float32)
    return np.ascontiguousarray(v.reshape(-1, 128).T)


def run(inputs, n_cores=8):
    f = lambda k: np.asarray(inputs[k], np.float32)
    x_prompt, x_sample = f("x_prompt"), f("x_sample")
    BATCH, SEQ, _ = x_prompt.shape
    DEC_BATCH = x_sample.shape[0]
    PAST = inputs["cache_ckv"].shape[2]
    TP = SEQ + 16
    NS = DEC_BATCH // n_cores
    if NS % 2:
        raise ValueError("NS must be even")
    nc = build_nc(TP, NS, PAST)
    consts = host_consts(TP, PAST)
    meta = f("meta_tokens")
    vecs = np.zeros((2, 128, NV), np.float32)
    for l in range(2):
        vecs[l, :, 0:8] = col128(f("norm_g")[l])
        vecs[l, :, 8:10] = col128(f("q_norm_g")[l])
        cw = f("conv_w")[l]
        for cc in range(2):
            for j in range(3):
                vecs[l, :, 10 + cc * 3 + j] = cw[j, cc * 128:(cc + 1) * 128]
        vecs[l, :, 16:23] = col128(f("shift_mu")[l])
        vecs[l, :, 23:25] = col128(f("decay_w0")[l])
        vecs[l, :, 25:27] = col128(f("iclr_a0")[l])
        vecs[l, :, 27:29] = col128(f("key_kk")[l])
        vecs[l, :, 29:31] = col128(f("key_ka")[l])
        vecs[l, :, 31:33] = col128(f("bonus_rk")[l])
    shared = dict(w_in=f("w_in"), w_out=f("w_out"), w_uq=f("w_uq"), w_ukv=f("w_ukv"), dw2=f("decay_w2"), a2=f("iclr_a2"),
                  vecs=vecs, kvg=f("kv_norm_g")[:, None, :].copy(), lnw=f("lnx_w")[:, None, :].copy(), lnb=f("lnx_b")[:, None, :].copy(),
                  fg=f("final_g")[None, :].copy(), **consts)
    in_maps = []
    for c in range(n_cores):
        pb = c % BATCH
        sl = slice(c * NS, (c + 1) * NS)
        m = dict(shared)
        m["xp"] = np.ascontiguousarray(np.concatenate([meta, x_prompt[pb]], axis=0))
        m["xs"] = np.ascontiguousarray(x_sample[sl].reshape(NS * 64, D))
        m["cckv"] = np.ascontiguousarray(f("cache_ckv")[:, sl])
        m["ckr"] = np.ascontiguousarray(f("cache_krope")[:, sl])
        m["sconvT"] = np.ascontiguousarray(f("state_conv")[:, sl].transpose(0, 1, 3, 2))
        m["sshiftT"] = np.ascontiguousarray(f("state_shift")[:, sl].reshape(2, NS, 7, 128).transpose(0, 1, 3, 2))
        m["swkv"] = np.ascontiguousarray(f("state_wkv")[:, sl].reshape(2, NS, 256, 64))
        in_maps.append(m)
    res = run_bass_kernel_spmd(nc, in_maps, core_ids=list(range(n_cores)))
    R = res.results
    y_prompt = np.stack([R[b]["y_p"] for b in range(BATCH)])
    y_sample = np.concatenate([R[c]["y_s"].reshape(NS, 64, D) for c in range(n_cores)], axis=0)
    ckv_p = np.stack([R[b]["ckv_p"] for b in range(BATCH)], axis=1)
    kr_p = np.stack([R[b]["kr_p"] for b in range(BATCH)], axis=1)
    conv_p = np.stack([R[b]["conv_p"].transpose(0, 2, 1) for b in range(BATCH)], axis=1)
    shift_p = np.stack([R[b]["shift_p"].transpose(0, 2, 1).reshape(2, 896) for b in range(BATCH)], axis=1)
    wkv_p = np.stack([R[b]["wkv_p"].reshape(2, 4, 64, 64) for b in range(BATCH)], axis=1)
    ckv_s = np.concatenate([R[c]["ckv_s"] for c in range(n_cores)], axis=1)
    kr_s = np.concatenate([R[c]["kr_s"] for c in range(n_cores)], axis=1)
    conv_s = np.concatenate([R[c]["conv_s"].transpose(0, 1, 3, 2) for c in range(n_cores)], axis=1)
    shift_s = np.concatenate([R[c]["shift_s"].transpose(0, 1, 3, 2).reshape(2, NS, 896) for c in range(n_cores)], axis=1)
    wkv_s = np.concatenate([R[c]["wkv_s"].reshape(2, NS, 4, 64, 64) for c in range(n_cores)], axis=1)
    outs = (y_prompt, y_sample, ckv_p, kr_p, conv_p, shift_p, wkv_p, ckv_s, kr_s, conv_s, shift_s, wkv_s)
    return tuple(np.ascontiguousarray(o, dtype=np.float32) for o in outs)


def kernel(**inputs):
    return run(inputs, n_cores=8)
```

```python
import contextlib
import numpy as np
import concourse.bass as bass
import concourse.mybir as mybir
from concourse.bass_utils import run_bass_kernel_spmd

F32 = mybir.dt.float32
BF16 = mybir.dt.bfloat16
F32R = mybir.dt.float32r


def R(ap):
    return ap.bitcast(F32R)

ALU = mybir.AluOpType
AF = mybir.ActivationFunctionType
AX = mybir.AxisListType

ENGS = ("pe", "act", "dve", "pool", "sp")


class Op:
    __slots__ = ("eng", "fn", "deps", "idx", "needs_inc", "cnt", "dma", "dsem", "dval", "rb")

    def __init__(self, eng, fn, dma):
        self.eng = eng
        self.fn = fn
        self.deps = []
        self.idx = -1
        self.needs_inc = False
        self.cnt = 0
        self.dma = dma
        self.dsem = None
        self.dval = 0


class Sched:
    NDSEM = 12

    def __init__(self, nc):
        self.nc = nc
        self.q = {e: [] for e in ENGS}
        self.lastw = {}
        self.readers = {}
        self.dma_rr = {e: 0 for e in ENGS}
        self.dma_last = {}
        self.dma_cnt = {}

    def add(self, eng, fn, reads=(), writes=(), dma=False, serial=False, rb=0):
        import os, sys
        self.nadd = getattr(self, "nadd", 0) + 1
        mx = int(os.environ.get("KMAXOPS", "0"))
        if mx and self.nadd > mx:
            return None
        if mx and self.nadd > mx - 3:
            f = sys._getframe(1)
            if f.f_code.co_name in ("DMA", "MM", "TR", "cast_scale"):
                f = f.f_back
            print("OP", self.nadd, eng, f.f_code.co_name, f.f_lineno, flush=True)
        op = Op(eng, fn, dma)
        op.rb = rb
        deps = {}
        pk = [r for r in reads if isinstance(r, str) and r.startswith("ps")]
        if pk:
            reads = [r for r in reads if r not in pk]
            writes = list(writes) + [r for r in pk if r not in writes]
        for r in reads:
            w = self.lastw.get(r)
            if w is not None:
                deps[id(w)] = w
        for wkey in writes:
            w = self.lastw.get(wkey)
            if w is not None:
                deps[id(w)] = w
            for rd in self.readers.get(wkey, ()):
                deps[id(rd)] = rd
        for wkey in writes:
            self.lastw[wkey] = op
            self.readers[wkey] = []
        for r in reads:
            self.readers.setdefault(r, []).append(op)
        if dma:
            slot = self.dma_rr[eng] % self.NDSEM
            self.dma_rr[eng] += 1
            key = (eng, slot)
            prev = self.dma_last.get(key)
            if prev is not None:
                deps[id(prev)] = prev
            self.dma_last[key] = op
            self.dma_cnt[key] = self.dma_cnt.get(key, 0) + 1
            op.dsem = key
            op.dval = 16 * self.dma_cnt[key]
        op.idx = len(self.q[eng])
        if serial and self.q[eng]:
            pv = self.q[eng][-1]
            op.deps.append(pv)
            pv.needs_inc = True
        self.q[eng].append(op)
        for d in deps.values():
            if d is op:
                continue
            if d.dma:
                op.deps.append(d)
            elif d.eng == eng:
                if dma:
                    op.deps.append(d)
                    d.needs_inc = True
                elif eng == "pe":
                    if d.rb != rb:
                        op.deps.append(d)
                        d.needs_inc = True
                    continue
                elif op.idx - d.idx <= 2:
                    op.deps.append(d)
                    d.needs_inc = True
            else:
                op.deps.append(d)
                d.needs_inc = True
        return op

    def emit(self):
        nc = self.nc
        with contextlib.ExitStack() as st:
            esem = {e: st.enter_context(nc.semaphore(f"s_{e}")) for e in ENGS}
            dsem = {}
            for e in ENGS:
                for s in range(min(self.NDSEM, self.dma_rr[e])):
                    dsem[(e, s)] = st.enter_context(nc.semaphore(f"d_{e}{s}"))
            for e in ENGS:
                c = 0
                for op in self.q[e]:
                    if op.needs_inc and not op.dma:
                        c += 1
                        op.cnt = c
            block = st.enter_context(nc.Block())

            def run(engname, eng):
                seen = {}
                for op in self.q[engname]:
                    for d in op.deps:
                        if d.dma:
                            sem, val = dsem[d.dsem], d.dval
                        else:
                            sem, val = esem[d.eng], d.cnt
                        k = id(sem)
                        if seen.get(k, 0) >= val:
                            continue
                        seen[k] = val
                        eng.wait_ge(sem, val)
                    ins = op.fn(eng)
                    if op.dma:
                        ins.then_inc(dsem[op.dsem], 16)
                    elif op.needs_inc:
                        ins.then_inc(esem[engname], 1)
                for (e, s), last in self.dma_last.items():
                    if e == engname:
                        eng.wait_ge(dsem[(e, s)], last.dval)

            if self.q["pe"]:
                @block.tensor
                def _(eng):
                    run("pe", eng)
            if self.q["act"]:
                @block.scalar
                def _(eng):
                    run("act", eng)
            if self.q["dve"]:
                @block.vector
                def _(eng):
                    run("dve", eng)
            if self.q["pool"]:
                @block.gpsimd
                def _(eng):
                    run("pool", eng)
            if self.q["sp"]:
                @block.sync
                def _(eng):
                    run("sp", eng)


D = 1024
IN_TOTAL = 3104
NV = 33
SCALE = float(96 ** -0.5)
LOGW_C = float(-np.exp(-0.5))
RMS_EPS = 1e-6
GN_EPS = 64e-5


def build_nc(TP, NS, PAST):
    nc = bass.Bass("TRN2", target_bir_lowering=False)
    NPT = 1 + (TP - 16) // 128
    NKT_S = PAST // 128 + 1
    NPOS = TP + 64

    def din(name, shape):
        return nc.dram_tensor(name, list(shape), F32, kind="ExternalInput").ap()

    def dout(name, shape):
        return nc.dram_tensor(name, list(shape), F32, kind="ExternalOutput").ap()

    xp = din("xp", [TP, D]); xs = din("xs", [NS * 64, D])
    cckv = din("cckv", [2, NS, PAST, 128]); ckr = din("ckr", [2, NS, PAST, 32])
    sconvT = din("sconvT", [2, NS, 256, 2]); sshiftT = din("sshiftT", [2, NS, 128, 7])
    swkv = din("swkv", [2, NS, 256, 64])
    w_in = din("w_in", [2, D, IN_TOTAL]); w_out = din("w_out", [2, D, D])
    w_uq = din("w_uq", [2, 256, 768]); w_ukv = din("w_ukv", [2, 128, 1024])
    dw2 = din("dw2", [2, 64, 256]); a2 = din("a2", [2, 64, 256])
    vecs = din("vecs", [2, 128, NV])
    kvg = din("kvg", [2, 1, 128]); lnw = din("lnw", [2, 1, 256]); lnb = din("lnb", [2, 1, 256])
    fg = din("fg", [1, D])
    c_ident = din("c_ident", [128, 128]); c_blk = din("c_blk", [128, 128])
    c_mg = din("c_mg", [128, 416]); c_sl = din("c_sl", [128, 128]); c_am = din("c_am", [128, 128])
    c_scan = din("c_scan", [128, 256])
    rope_tm = din("rope_tm", [NPOS, 32]); rope_fm = din("rope_fm", [32, 2, NPOS])

    y_p = dout("y_p", [TP - 16, D]); y_s = dout("y_s", [NS * 64, D])
    ckv_p = dout("ckv_p", [2, TP, 128]); kr_p = dout("kr_p", [2, TP, 32])
    conv_p = dout("conv_p", [2, 256, 2]); shift_p = dout("shift_p", [2, 128, 7]); wkv_p = dout("wkv_p", [2, 256, 64])
    ckv_s = dout("ckv_s", [2, NS, 64, 128]); kr_s = dout("kr_s", [2, NS, 64, 32])
    conv_s = dout("conv_s", [2, NS, 256, 2]); shift_s = dout("shift_s", [2, NS, 128, 7]); wkv_s = dout("wkv_s", [2, NS, 256, 64])
    scr_p = nc.dram_tensor("scr_p", [TP, D], F32, kind="Internal").ap()
    scr_s = nc.dram_tensor("scr_s", [NS * 64, D], F32, kind="Internal").ap()

    with contextlib.ExitStack() as st:
        def sb(name, shape, dt=F32):
            return st.enter_context(nc.sbuf_tensor(name, list(shape), dt))

        def pst(name):
            return st.enter_context(nc.psum_tensor(name, [128, 512], F32))

        S = Sched(nc)
        defer = [None]

        def A(*a_, **k_):
            if defer[0] is not None:
                defer[0].append((a_, k_))
                return None
            return S.add(*a_, **k_)
        PS = [pst(f"ps{i}") for i in range(8)]

        W_in = sb("W_in", [128, 8, IN_TOTAL], BF16)
        W_out = sb("W_out", [128, 8, D], BF16)
        W_uq = sb("W_uq", [128, 2, 768], BF16)
        W_uqsw = sb("W_uqsw", [128, 2, 256], BF16)
        W_ukT = sb("W_ukT", [64, 8, 128], BF16)
        W_uvp = sb("W_uvp", [128, 8, 128], BF16)
        dwa = sb("dwa", [128, 256])
        vec = sb("vec", [128, NV]); omka = sb("omka", [128, 2]); bonmat = sb("bonmat", [128, 2, 4])
        kvg_bc = sb("kvg_bc", [128, 128]); lnw_bc = sb("lnw_bc", [128, 256]); lnb_bc = sb("lnb_bc", [128, 256])
        fg_bc = sb("fg_bc", [128, D])
        ident = sb("ident", [128, 128]); identr = sb("identr", [128, 128]); identb = sb("identb", [128, 128], BF16); hbf = sb("hbf", [128, D], BF16); ycb = sb("ycb", [128, 256], BF16); blk = sb("blk", [128, 128]); mg = sb("mg", [128, 416]); msl = sb("msl", [128, 128])
        amf = sb("amf", [128, 128]); amask = sb("amask", [128, 128], BF16); scanm = sb("scanm", [128, 256])
        ones_bf = sb("ones_bf", [128, 128], BF16)
        stage = sb("stage", [128, 2, 1024])
        CKVT_p = sb("CKVT_p", [128, TP], BF16); CKV_p = sb("CKV_p", [128, NPT, 129], BF16); KRT_p = sb("KRT_p", [32, TP], BF16)
        CKVT_s = sb("CKVT_s", [128, PAST], BF16); CKV_s = sb("CKV_s", [128, PAST // 128, 129], BF16)
        KRT_s = sb("KRT_s", [32, PAST], BF16)
        CKVT_o = sb("CKVT_o", [128, 2, 64], BF16); CKV_o = sb("CKV_o", [64, 2, 129], BF16); KRT_o = sb("KRT_o", [32, 2, 64], BF16)
        ux = sb("ux", [128, 2, 132]); uxcar = sb("uxcar", [128, 3, 2, 2]); zcx = sb("zcx", [128, 7, 130]); zcar = sb("zcar", [128, 3, 7])
        Hp = sb("Hp", [128, 3, 2, 128])
        xt = sb("xt", [128, 2, D]); hfb = sb("hfb", [128, 2, D]); hT = sb("hT", [128, 8, 128], BF16)
        catT = sb("catT", [128, 8, 128], BF16)
        ss = sb("ss", [128, 8]); rs_ = sb("rs_", [128, 8]); epsb = sb("epsb", [128, 2])
        cqn = sb("cqn", [128, 256]); cqnT = sb("cqnT", [128, 2, 128], BF16)
        ckvn = sb("ckvn", [128, 128]); krr = sb("krr", [128, 32]); ropeA = sb("ropeA", [128, 32]); ropeB = sb("ropeB", [128, 32])
        cs_tm2 = sb("cs_tm2", [128, 2, 32]); cs_fm2 = sb("cs_fm2", [32, 2, 2, 128])
        sgc = sb("sgc", [128, 2, 256])
        xin_sb = sb("xin_sb", [128, 128]); cv1 = sb("cv1", [128, 128]); sga = sb("sga", [128, 128])
        sgb = sb("sgb", [128, 4, 128])
        zs = sb("zs", [128, 7, 128])
        QRT = sb("QRT", [32, 8, 128], BF16); QaT = sb("QaT", [128, 8, 128], BF16)
        PT = sb("PT", [128, 2, 512], BF16); yraw = sb("yraw", [128, 256]); rden = sb("rden", [128, 2, 4]); otok3 = sb("otok3", [128, 2, 3, 128], BF16); OaT = sb("OaT", [128, 8, 128], BF16)
        QnT = OaT
        th = sb("th", [64, 128]); lw = sb("lw", [128, 2, 128]); av = sb("av", [128, 2, 128]); G = sb("G", [128, 2, 128])
        Epos = sb("Epos", [128, 2, 128]); Eneg = sb("Eneg", [128, 2, 128]); Eprev = sb("Eprev", [128, 2, 128])
        kk = sb("kk", [128, 2, 128]); kmod = sb("kmod", [128, 2, 128]); bb = sb("bb", [128, 2, 128])
        rt1 = Eprev
        AR = sb("AR", [128, 2, 256], BF16); BTb = sb("BTb", [128, 2, 128], BF16); KTb = sb("KTb", [128, 2, 128], BF16); Vtokb = sb("Vtokb", [128, 2, 256], BF16); Hb = sb("Hb", [128, 128], BF16); BT = sb("BT", [128, 2, 128]); KT = sb("KT", [128, 2, 128]); rkp = G
        rkb = sb("rkb", [128, 2, 4])
        Btok = sb("Btok", [128, 2, 256], BF16); Ktok = sb("Ktok", [128, 2, 256], BF16); Vtok = sb("Vtok", [128, 2, 256])
        Gm = sb("Gm", [128, 2, 512], BF16)
        NX = sb("NX", [128, 2, 2, 128], BF16); NXT = sb("NXT", [128, 2, 2, 128], BF16); NW = sb("NW", [128, 2, 2, 128], BF16)
        X1 = sb("X1", [128, 128], BF16); U = sb("U", [128, 128], BF16)
        yc = sb("yc", [128, 256]); gst = sb("gst", [128, 8]); ybv = sb("ybv", [128, 256])
        ysq = ybv
        Sp = sb("Sp", [128, 128]); wko = sb("wko", [128, 128])

        dq = [0]

        dma_sp_only = [False]

        def DMA(out, in_, reads=(), writes=()):
            e = "sp" if (dq[0] % 2 == 0 or dma_sp_only[0]) else "pool"
            dq[0] += 1
            A(e, lambda g, o=out, i=in_: g.dma_start(out=o, in_=i), reads=reads, writes=writes, dma=True)

        def MM(ps_key, mms, reads):
            def fn(g, mms=mms):
                ins = None
                for i, (o, l, r) in enumerate(mms):
                    ins = g.matmul(o, lhsT=l, rhs=r, start=(i == 0), stop=(i == len(mms) - 1))
                return ins
            A("pe", fn, reads=reads, writes=[ps_key])

        def TR(ps_key, outs_ins, reads, bf=False):
            idt_ = identb if bf else ident

            def fn(g, oi=outs_ins):
                ins = None
                for (o, i_) in oi:
                    ins = g.transpose(o, i_, idt_[0:i_.shape[0], 0:i_.shape[0]])
                return ins
            A("pe", fn, reads=list(reads) + ["ident"], writes=[ps_key])

        def PSB(i):
            return PS[i][:].bitcast(BF16)

        def PE1(out, lhsT, rhs, start, stop, reads, writes, r32=False, rb=0):
            if r32:
                lhsT, rhs = R(lhsT), R(rhs)
            A("pe", lambda g: g.matmul(out, lhsT=lhsT, rhs=rhs, start=start, stop=stop), reads=reads, writes=writes, rb=rb)

        DMA(ident[:], c_ident[:, :], writes=["ident"]); DMA(blk[:], c_blk[:, :], writes=["blk"])
        DMA(mg[:], c_mg[:, :], writes=["mg"]); DMA(msl[:], c_sl[:, :], writes=["msl"])
        DMA(amf[:], c_am[:, :], writes=["amf"]); DMA(scanm[:], c_scan[:, :], writes=["scanm"])
        DMA(fg_bc[:], fg.partition_broadcast(128), writes=["fg_bc"])
        A("dve", lambda g: g.tensor_copy(amask[:], amf[:]), reads=["amf"], writes=["amask"])
        A("dve", lambda g: g.tensor_copy(R(identr[:]), ident[:]), reads=["ident"], writes=["identr"])
        A("dve", lambda g: g.tensor_copy(identb[:], ident[:]), reads=["ident"], writes=["ident"])
        A("pool", lambda g: g.memset(ones_bf[:], 1.0), writes=["ones_bf"])
        A("pool", lambda g: g.memset(CKV_p[:], 1.0), writes=["kvp"])
        A("pool", lambda g: g.memset(CKV_s[:], 1.0), writes=["kvs"])
        A("pool", lambda g: g.memset(CKV_o[:], 1.0), writes=["kvo0", "kvo1"])

        cast_rr = [0]

        def cast_scale(out, in_, scal, reads, writes):
            i = (0, 2)[cast_rr[0] % 2]
            cast_rr[0] += 1
            if scal is None:
                if i == 0:
                    A("dve", lambda g: g.tensor_copy(out, in_), reads=reads, writes=writes)
                elif i == 1:
                    A("pool", lambda g: g.tensor_copy(out, in_), reads=reads, writes=writes)
                else:
                    A("act", lambda g: g.copy(out, in_), reads=reads, writes=writes)
            else:
                if i == 0:
                    A("dve", lambda g: g.tensor_scalar(out, in_, scal, None, ALU.mult), reads=reads, writes=writes)
                elif i == 1:
                    A("pool", lambda g: g.tensor_scalar(out, in_, scal, None, ALU.mult), reads=reads, writes=writes)
                else:
                    A("act", lambda g: g.activation(out, in_, AF.Copy, scale=scal), reads=reads, writes=writes)

        stg = [0]

        def stage_load(src, ncols):
            b = stg[0] % 2
            stg[0] += 1
            DMA(stage[:, b, 0:ncols], src, writes=[f"stage{b}"])
            return b

        def load_weights(l):
            dma_sp_only[0] = True
            DMA(vec[:], vecs[l], writes=["vec"])
            DMA(kvg_bc[:], kvg[l].partition_broadcast(128), writes=["kvg_bc"])
            DMA(lnw_bc[:], lnw[l].partition_broadcast(128), writes=["lnw_bc"])
            DMA(lnb_bc[:], lnb[l].partition_broadcast(128), writes=["lnb_bc"])
            DMA(dwa[0:64, :], dw2[l], writes=["dwa"]); DMA(dwa[64:128, :], a2[l], writes=["dwa"])
            A("dve", lambda g: g.tensor_scalar(omka[:], vec[:, 29:31], -1.0, 1.0, ALU.mult, ALU.add), reads=["vec"], writes=["omka"])
            A("pool", lambda g: g.memset(bonmat[:], 0.0), writes=["bonmat"])
            for c in range(2):
                A("pool", lambda g, c=c: g.tensor_copy(bonmat[0:64, c, 2 * c:2 * c + 1], vec[0:64, 31 + c:32 + c]), reads=["vec"], writes=["bonmat"])
                A("pool", lambda g, c=c: g.tensor_copy(bonmat[64:128, c, 2 * c + 1:2 * c + 2], vec[64:128, 31 + c:32 + c]), reads=["vec"], writes=["bonmat"])
            for k in range(8):
                for c0 in range(0, IN_TOTAL, 1024):
                    nco = min(1024, IN_TOTAL - c0)
                    b = stage_load(w_in[l, k * 128:(k + 1) * 128, c0:c0 + nco], nco)
                    cast_scale(W_in[:, k, c0:c0 + nco], stage[:, b, 0:nco], vec[:, k:k + 1], [f"stage{b}", "vec"], ["W_in"])
            for k in range(8):
                b = stage_load(w_out[l, k * 128:(k + 1) * 128, :], 1024)
                cast_scale(W_out[:, k, :], stage[:, b, 0:1024], None, [f"stage{b}"], ["W_out"])
            for k in range(2):
                b = stage_load(w_uq[l, k * 128:(k + 1) * 128, :], 768)
                cast_scale(W_uq[:, k, :], stage[:, b, 0:768], vec[:, 8 + k:9 + k], [f"stage{b}", "vec"], ["W_uq"])
            wv = W_uq[:].rearrange("p k (h e) -> p k h e", h=8)
            wsw = W_uqsw[:].rearrange("p k (h e) -> p k h e", h=8)
            for k in range(2):
                A("dve", lambda g, k=k: g.tensor_scalar(wsw[:, k, :, 0:16], wv[:, k, :, 80:96], -1.0, None, ALU.mult), reads=["W_uq"], writes=["W_uqsw"])
                A("pool", lambda g, k=k: g.tensor_copy(wsw[:, k, :, 16:32], wv[:, k, :, 64:80]), reads=["W_uq"], writes=["W_uqsw"])
            b = stage_load(w_ukv[l], 1024)
            A("pool", lambda g: g.memset(W_uvp[:], 0.0), writes=["W_uvp"])
            for h in range(8):
                o = 64 * (h % 2)
                cast_scale(W_uvp[:, h, o:o + 64], stage[:, b, h * 128 + 64:h * 128 + 128], None, [f"stage{b}"], ["W_uvp"])
            for hh in range(2):
                TR("ps0", [(PS[0][:, j * 128:(j + 1) * 128], stage[:, b, (hh * 4 + j) * 128:(hh * 4 + j + 1) * 128]) for j in range(4)], [f"stage{b}"])
                A("dve", lambda g, hh=hh: g.tensor_copy(W_ukT[:, hh * 4:hh * 4 + 4, :], PS[0][0:64, :].rearrange("p (j e) -> p j e", j=4)), reads=["ps0"], writes=["W_ukT"])
            dma_sp_only[0] = False

        def rstd_from_ss(n, col, dim, eps):
            ec = {RMS_EPS: 0, GN_EPS: 1}[eps]
            A("act", lambda g: g.activation(rs_[0:n, col:col + 1], ss[0:n, col:col + 1], AF.Ln, bias=epsb[0:n, ec:ec + 1], scale=1.0 / dim), reads=["ss", "epsb"], writes=["rs_"])
            A("act", lambda g: g.activation(rs_[0:n, col:col + 1], rs_[0:n, col:col + 1], AF.Exp, scale=-0.5), reads=["rs_"], writes=["rs_"])

        def load_rope(t):
            par = t["par"]
            sg0 = t["segs"][0]
            C = sg0["C"]
            sv = dma_sp_only[0]
            dma_sp_only[0] = True
            DMA(cs_tm2[0:C, par, :], rope_tm[sg0["pos0"]:sg0["pos0"] + C, :], writes=[f"cs_tm{par}"])
            for sg in t["segs"]:
                DMA(cs_fm2[:, par, :, sg["col0"]:sg["col0"] + sg["C"]], rope_fm[:, :, sg["pos0"]:sg["pos0"] + sg["C"]], writes=[f"cs_fm{par}"])
            dma_sp_only[0] = sv

        def load_x(t):
            par = t["par"]
            rd = ["scr"] if t["l"] == 1 else []
            sv = dma_sp_only[0]
            dma_sp_only[0] = True
            DMA(xt[0:t["n"], par, :], t["x_src"], reads=rd, writes=[f"xt{par}"])
            dma_sp_only[0] = sv

        def front1a(t):
            n, par = t["n"], t["par"]
            X = xt[:, par, :]
            xk = f"xt{par}"
            hf = hfb[:, par, :]
            hk_ = f"hf{par}"
            A("dve", lambda g: g.memset(ss[:, 0:1], 0.0), writes=["ss"])
            A("act", lambda g: g.activation(hf[0:n, :], X[0:n, :], AF.Square, accum_out=ss[0:n, 0:1]), reads=[xk, "ss"], writes=[hk_, "ss"])
            rstd_from_ss(n, 0, D, RMS_EPS)
            A("dve", lambda g: g.tensor_scalar(hbf[0:n, :], X[0:n, :], rs_[0:n, 0:1], None, ALU.mult), reads=[xk, "rs_"], writes=["hbf"])

        def front1b(t):
            n, par = t["n"], t["par"]
            hf = hfb[:, par, :]
            hk_ = f"hf{par}"
            for hh in range(2):
                TR(f"ps{hh}", [(PSB(hh)[:, j * 128:j * 128 + n], hbf[0:n, (hh * 4 + j) * 128:(hh * 4 + j + 1) * 128]) for j in range(4)], ["hbf"], bf=True)
                src = PSB(hh)[:, 0:512].rearrange("p (j e) -> p j e", j=4)[:, :, 0:n]
                if hh == 0:
                    A("act", lambda g, src=src: g.copy(hT[:, 0:4, 0:n], src), reads=["ps0"], writes=["hT"])
                else:
                    A("dve", lambda g, src=src: g.tensor_copy(hT[:, 4:8, 0:n], src), reads=["ps1"], writes=["hT"])

        def do_tile(t):
            l, x_dst, y_dst, segs, n, par = t["l"], t["x_dst"], t["y_dst"], t["segs"], t["n"], t["par"]
            nseg = len(segs)
            X = xt[:, par, :]
            xk = f"xt{par}"
            C0 = segs[0]["C"]
            hf = hfb[:, par, :]
            hfk = f"hf{par}"
            cs_tm = cs_tm2[:, par, :]
            cs_fm = cs_fm2[:, par, :, :]
            ctk, cfk = f"cs_tm{par}", f"cs_fm{par}"
            qt1 = hf[0:32, 0:512].rearrange("p (j e) -> p j e", j=4)
            qt2 = hf[0:32, 512:1024].rearrange("p (j e) -> p j e", j=4)
            cst2 = hf[:, 0:256]

            def segview(ap2d):
                return ap2d.rearrange("p (s c) -> p s c", s=nseg)
            if not t.get('front_done'):
                front1a(t)
                front1b(t)
            def step2A(si, sg):
                c0, C = sg['col0'], sg['C']
                MM("ps2", [(PS[2][0:C, 0:416], hT[:, k, c0:c0 + C], W_in[:, k, 1024:1440]) for k in range(8)], ["hT", "W_in"])
                MM("ps3", [(PS[3][0:C, 0:256], hT[:, k, c0:c0 + C], W_in[:, k, 2848:3104]) for k in range(8)], ["hT", "W_in"])
                A("dve", lambda g: g.memset(ss[:, 1:3], 0.0), writes=["ss"])
                A("act", lambda g, C=C: g.activation(cst2[0:C, :], PS[2][0:C, 0:256], AF.Square, accum_out=ss[0:C, 1:2]), reads=["ps2", "ss"], writes=[hfk, "ss"])
                A("act", lambda g, C=C: g.activation(cst2[0:C, 0:128], PS[2][0:C, 256:384], AF.Square, accum_out=ss[0:C, 2:3]), reads=["ps2", "ss"], writes=[hfk, "ss"])
                rstd_from_ss(C, 1, 256, RMS_EPS)
                rstd_from_ss(C, 2, 128, RMS_EPS)
                A("dve", lambda g, C=C: g.tensor_scalar(cqn[0:C, :], PS[2][0:C, 0:256], rs_[0:C, 1:2], None, ALU.mult), reads=["ps2", "rs_"], writes=["cqn"])
                A("dve", lambda g, C=C: g.scalar_tensor_tensor(ckvn[0:C, :], PS[2][0:C, 256:384], rs_[0:C, 2:3], kvg_bc[0:C, :], ALU.mult, ALU.mult), reads=["ps2", "rs_", "kvg_bc"], writes=["ckvn"])
                krv = PS[2][0:C, 384:416].rearrange("p (t e) -> p t e", t=2)
                A("dve", lambda g, C=C, krv=krv: g.tensor_tensor(ropeA[0:C, :].rearrange("p (t e) -> p t e", t=2), krv, cs_tm[0:C, 0:16].unsqueeze(1).broadcast_to([C, 2, 16]), ALU.mult), reads=["ps2", ctk], writes=["ropeA"])
                A("dve", lambda g, C=C, krv=krv: g.tensor_tensor(ropeB[0:C, :].rearrange("p (t e) -> p t e", t=2), krv, cs_tm[0:C, 16:32].unsqueeze(1).broadcast_to([C, 2, 16]), ALU.mult), reads=["ps2", ctk], writes=["ropeB"])
                A("dve", lambda g, C=C: g.tensor_tensor(krr[0:C, 0:16], ropeA[0:C, 0:16], ropeB[0:C, 16:32], ALU.subtract), reads=["ropeA", "ropeB"], writes=["krr"])
                A("dve", lambda g, C=C: g.tensor_tensor(krr[0:C, 16:32], ropeB[0:C, 0:16], ropeA[0:C, 16:32], ALU.add), reads=["ropeA", "ropeB"], writes=["krr"])
                A("act", lambda g, C=C, si=si: g.activation(sgc[0:C, si, :], PS[3][0:C, 0:256], AF.Silu), reads=["ps3"], writes=["sgc"])
                DMA(sg["ckv_out"], ckvn[0:C, :], reads=["ckvn"])
                DMA(sg["kr_out"], krr[0:C, :], reads=["krr"])
            def step2B(si, sg):
                c0, C = sg['col0'], sg['C']
                TR("ps0", [(PS[0][:, 0:C], cqn[0:C, 0:128]), (PS[0][:, 128:128 + C], cqn[0:C, 128:256]),
                           (PS[0][:, 256:256 + C], ckvn[0:C, :]), (PS[0][0:32, 384:384 + C], krr[0:C, :])], ["cqn", "ckvn", "krr"])
                A("dve", lambda g, C=C, c0=c0: g.tensor_copy(cqnT[:, :, c0:c0 + C], PS[0][:, 0:256].rearrange("p (j e) -> p j e", j=2)[:, :, 0:C]), reads=["ps0"], writes=["cqnT"])
                A("act", lambda g, C=C, sg=sg: g.copy(sg["CKVTown"], PS[0][:, 256:256 + C]), reads=["ps0"], writes=[sg["ownkey"]])
                A("act", lambda g, C=C, sg=sg: g.copy(sg["KRTown"], PS[0][0:32, 384:384 + C]), reads=["ps0"], writes=[sg["ownkey"]])
                A("pool", lambda g, C=C, sg=sg: g.tensor_copy(sg["CKVown"], ckvn[0:C, :]), reads=["ckvn"], writes=[sg["ownkey"]])
            def step3():
                slot = [0]

                def fm(c0):
                    bank = (4, 5, 6, 7)[slot[0] % 4]
                    slot[0] += 1
                    key = f"ps{bank}"
                    MM(key, [(PS[bank][:, 0:n], W_in[:, k, c0:c0 + 128], hT[:, k, 0:n]) for k in range(8)], ["hT", "W_in"])
                    return PS[bank][:, 0:n], key
                W1 = nseg * (C0 + 2)
                for cc in range(2):
                    uxv = ux[:, cc, 0:W1].rearrange("p (s c) -> p s c", s=nseg)
                    for si, sg in enumerate(segs):
                        A("pool", lambda g, sg=sg, cc=cc, si=si, uxv=uxv: g.tensor_copy(uxv[:, si, 0:2], uxcar[:, sg["slot"], cc, :]), reads=["uxcar"], writes=["ux"])
                    p_x, kx = fm(cc * 128)
                    A("act", lambda g, p_x=p_x: g.copy(xin_sb[:, 0:n], p_x), reads=[kx], writes=["xin_sb"])
                    p_c, kc_ = fm(512 + cc * 128)
                    A("dve", lambda g, p_c=p_c, uxv=uxv: g.tensor_tensor(uxv[:, :, 2:2 + C0], segview(p_c), segview(xin_sb[:, 0:n]), ALU.mult), reads=[kc_, "xin_sb"], writes=["ux"])
                    cvv = segview(cv1[:, 0:n])
                    A("dve", lambda g, uxv=uxv, cvv=cvv, cc=cc: g.tensor_scalar(cvv, uxv[:, :, 0:C0], vec[:, 10 + cc * 3:11 + cc * 3], None, ALU.mult), reads=["ux", "vec"], writes=["cv1"])
                    A("dve", lambda g, uxv=uxv, cvv=cvv, cc=cc: g.scalar_tensor_tensor(cvv, uxv[:, :, 1:1 + C0], vec[:, 11 + cc * 3:12 + cc * 3], cvv, ALU.mult, ALU.add), reads=["ux", "vec", "cv1"], writes=["cv1"])
                    A("dve", lambda g, uxv=uxv, cvv=cvv, cc=cc: g.scalar_tensor_tensor(cvv, uxv[:, :, 2:2 + C0], vec[:, 12 + cc * 3:13 + cc * 3], cvv, ALU.mult, ALU.add), reads=["ux", "vec", "cv1"], writes=["cv1"])
                    p_g, kg = fm(768 + cc * 128)
                    A("act", lambda g, p_g=p_g: g.activation(sga[:, 0:n], p_g, AF.Silu), reads=[kg], writes=["sga"])
                    A("dve", lambda g: g.tensor_tensor(cv1[:, 0:n], cv1[:, 0:n], sga[:, 0:n], ALU.mult), reads=["cv1", "sga"], writes=["cv1"])
                    p_b, kb = fm(256 + cc * 128)
                    A("dve", lambda g, p_b=p_b, cc=cc: g.tensor_tensor(catT[:, cc, 0:n], p_b, cv1[:, 0:n], ALU.mult), reads=[kb, "cv1"], writes=["catT_a"])
                    for si, sg in enumerate(segs):
                        if sg["last"]:
                            DMA(sg["conv_out"][cc * 128:(cc + 1) * 128, :], uxv[:, si, C0:C0 + 2], reads=["ux"])
                        else:
                            A("pool", lambda g, sg=sg, cc=cc, si=si, uxv=uxv: g.tensor_copy(uxcar[:, sg["slot"], cc, :], uxv[:, si, C0:C0 + 2]), reads=["ux"], writes=["uxcar"])
                for j in range(4):
                    p_, k_ = fm(1440 + j * 128)
                    A("act", lambda g, p_=p_, j=j: g.activation(sgb[:, j, 0:n], p_, AF.Silu), reads=[k_], writes=["sgb"])
                W2 = nseg * (C0 + 1)
                zv4 = zcx[:, :, 0:W2].rearrange("p j (s c) -> p j s c", s=nseg)
                for si, sg in enumerate(segs):
                    A("pool", lambda g, sg=sg, si=si: g.tensor_copy(zv4[:, :, si, 0], zcar[:, sg["slot"], :]), reads=["zcar"], writes=["zcx"])
                if nseg == 1 and t.get("mid2b") is not None:
                    t["mid2b"]()
                for j in range(7):
                    p_, k_ = fm(1952 + j * 128)
                    zv = zcx[:, j, 0:W2].rearrange("p (s c) -> p s c", s=nseg)
                    if j % 2 == 0:
                        A("act", lambda g, p_=p_, zv=zv: g.copy(zv[:, :, 1:1 + C0], segview(p_)), reads=[k_], writes=["zcx"])
                    else:
                        A("dve", lambda g, p_=p_, zv=zv: g.tensor_copy(zv[:, :, 1:1 + C0], segview(p_)), reads=[k_], writes=["zcx"])
                zs4 = zs[:, :, 0:n].rearrange("p j (s c) -> p j s c", s=nseg)
                mub = vec[:, 16:23].unsqueeze(2).unsqueeze(3).broadcast_to([128, 7, nseg, C0])
                A("dve", lambda g: g.tensor_tensor(zs4, zv4[:, :, :, 0:C0], zv4[:, :, :, 1:1 + C0], ALU.subtract), reads=["zcx"], writes=["zs"])
                A("dve", lambda g: g.tensor_tensor(zs4, zs4, mub, ALU.mult), reads=["zs", "vec"], writes=["zs"])
                A("dve", lambda g: g.tensor_tensor(zs4, zs4, zv4[:, :, :, 1:1 + C0], ALU.add), reads=["zs", "zcx"], writes=["zs"])
                for si, sg in enumerate(segs):
                    A("pool", lambda g, sg=sg, si=si: g.tensor_copy(zcar[:, sg["slot"], :], zv4[:, :, si, C0]), reads=["zcx"], writes=["zcar"])
                for sg in segs:
                    if sg["last"]:
                        DMA(sg["shift_out"], zcar[:, sg["slot"], :], reads=["zcar"])
            if nseg == 1:
                step2A(0, segs[0])
                t["mid2b"] = lambda: step2B(0, segs[0])
                step3()
            else:
                for si, sg in enumerate(segs):
                    step2A(si, sg)
                    step2B(si, sg)
                step3()
            for h4 in range(2):
                for j in range(4):
                    h = h4 * 4 + j
                    MM("ps2", [(PS[2][0:64, j * 128:j * 128 + n], W_uq[:, k, h * 96:h * 96 + 64], cqnT[:, k, 0:n]) for k in range(2)], ["W_uq", "cqnT"])
                A("act", lambda g, h4=h4: g.copy(QnT[0:64, h4 * 4:h4 * 4 + 4, 0:n], PS[2][0:64, :].rearrange("p (j e) -> p j e", j=4)[:, :, 0:n]), reads=["ps2"], writes=["OaT"])
                for j in range(4):
                    h = h4 * 4 + j
                    MM("ps3", [(PS[3][0:32, j * 128:j * 128 + n], W_uq[:, k, h * 96 + 64:h * 96 + 96], cqnT[:, k, 0:n]) for k in range(2)], ["W_uq", "cqnT"])
                for j in range(4):
                    h = h4 * 4 + j
                    MM("ps0", [(PS[0][0:32, j * 128:j * 128 + n], W_uqsw[:, k, h * 32:h * 32 + 32], cqnT[:, k, 0:n]) for k in range(2)], ["W_uqsw", "cqnT"])
                cosb = cs_fm[:, 0, 0:n].unsqueeze(1).broadcast_to([32, 4, n])
                sinb = cs_fm[:, 1, 0:n].unsqueeze(1).broadcast_to([32, 4, n])
                A("dve", lambda g, cosb=cosb: g.tensor_tensor(qt1[:, :, 0:n], PS[3][0:32, :].rearrange("p (j e) -> p j e", j=4)[:, :, 0:n], cosb, ALU.mult), reads=["ps3", cfk], writes=[hfk])
                A("dve", lambda g, sinb=sinb: g.tensor_tensor(qt2[:, :, 0:n], PS[0][0:32, :].rearrange("p (j e) -> p j e", j=4)[:, :, 0:n], sinb, ALU.mult), reads=["ps0", cfk], writes=[hfk])
                A("dve", lambda g, h4=h4: g.tensor_tensor(QRT[:, h4 * 4:h4 * 4 + 4, 0:n], qt1[:, :, 0:n], qt2[:, :, 0:n], ALU.add), reads=[hfk], writes=["QRT"])
                for j in range(4):
                    h = h4 * 4 + j
                    MM("ps1", [(PS[1][:, j * 128:j * 128 + n], W_ukT[0:64, h, :], QnT[0:64, h, 0:n])], ["W_ukT", "OaT"])
                A("dve", lambda g, h4=h4: g.tensor_copy(QaT[:, h4 * 4:h4 * 4 + 4, 0:n], PS[1][:].rearrange("p (j e) -> p j e", j=4)[:, :, 0:n]), reads=["ps1"], writes=["QaT"])
            if t.get('next') is not None:
                front1a(t['next'])
            OB = [(6, 0), (6, 1), (6, 2), (7, 0), (7, 1), (7, 2), (2, 0), (2, 1)]
            SB = [3, 4, 5]

            def gen_attn():
                for sg in segs:
                    c0, C = sg["col0"], sg["C"]
                    if sg.get("preattn") is not None:
                        sg["preattn"]()
                        yield
                    kts = sg["ktiles"]
                    units = [(ki, hf_) for ki in range(len(kts)) for hf_ in range(2)]

                    def QK(u):
                        ki, hf_ = units[u]
                        ckvt_ap, krt_ap, ckv_ap, kn, diag, kkey = kts[ki]
                        bk = SB[u % 3]
                        o = PS[bk][0:kn, 0:4 * C].rearrange("p (h c) -> p h c", h=4)
                        MM(f"ps{bk}", [(o, ckvt_ap, QaT[:, 4 * hf_:4 * hf_ + 4, c0:c0 + C]),
                                       (o, krt_ap, QRT[0:32, 4 * hf_:4 * hf_ + 4, c0:c0 + C])], [kkey, "QaT", "QRT"])
                    def onorm(bi, ob, h0, nh, C=C, c0=c0):
                        tb = ob
                        ov = PS[ob][0:C, 0:nh * 132].rearrange("p (s e) -> p s e", e=132)
                        ot = otok3[0:C, bi % 2, 0:nh, :]
                        A("dve", lambda g: g.reciprocal(rden[0:C, bi % 2, 0:nh].unsqueeze(2), ov[:, :, 128:129]), reads=[f"ps{ob}"], writes=[f"rden{bi % 2}"])
                        A("dve", lambda g: g.tensor_tensor(ot, ov[:, :, 0:128], rden[0:C, bi % 2, 0:nh].unsqueeze(2).broadcast_to([C, nh, 128]), ALU.mult), reads=[f"ps{ob}", f"rden{bi % 2}"], writes=[f"otok{bi % 2}"])
                        TR(f"ps{tb}", [(PSB(tb)[:, j * 128:j * 128 + C], otok3[0:C, bi % 2, j, :]) for j in range(nh)], [f"otok{bi % 2}"], bf=True)
                        A("act", lambda g: g.copy(OaT[:, h0:h0 + nh, c0:c0 + C], PSB(tb)[:, 0:nh * 128].rearrange("p (j e) -> p j e", e=128)[:, :, 0:C]), reads=[f"ps{tb}"], writes=["OaT"])
                    QK(0)
                    started = set()
                    for u, (ki, hf_) in enumerate(units):
                        ckvt_ap, krt_ap, ckv_ap, kn, diag, kkey = kts[ki]
                        bk = SB[u % 3]
                        pb = u % 2
                        if u + 1 < len(units):
                            QK(u + 1)
                        A("act", lambda g, pb=pb, kn=kn, C=C, bk=bk: g.activation(PT[0:kn, pb, 0:4 * C], PS[bk][0:kn, 0:4 * C], AF.Exp, scale=SCALE), reads=[f"ps{bk}"], writes=[f"PT{pb}"])
                        if diag and C == 128:
                            A("dve", lambda g, pb=pb: g.tensor_tensor(PT[:, pb, :].rearrange("p (h c) -> p h c", h=4), PT[:, pb, :].rearrange("p (h c) -> p h c", h=4),
                                                                       amask[:].unsqueeze(1).broadcast_to([128, 4, 128]), ALU.mult), reads=[f"PT{pb}", "amask"], writes=[f"PT{pb}"])
                        last = (ki == len(kts) - 1)
                        for j in range(4):
                            h = 4 * hf_ + j
                            ob, sl = OB[h]
                            first = ob not in started
                            started.add(ob)
                            A("pe", lambda g, ckv_ap=ckv_ap, kn=kn, pb=pb, C=C, ob=ob, sl=sl, j=j, first=first, last=last:
                              g.matmul(PS[ob][0:C, sl * 132:sl * 132 + 129], lhsT=PT[0:kn, pb, j * C:(j + 1) * C], rhs=ckv_ap, start=first, stop=last, skip_group_check=True),
                              reads=[kkey, f"PT{pb}"], writes=[f"ps{ob}"])
                        if last and hf_ == 0:
                            onorm(0, 6, 0, 3)
                        yield
                    for bi, (ob, h0, nh) in ((1, (7, 3, 3)), (2, (2, 6, 2))):
                        onorm(bi, ob, h0, nh)
                        yield
                for hp in range(4):
                    tb = 4 + hp % 2
                    MM(f"ps{tb}", [(PS[tb][:, 0:n], W_uvp[:, 2 * hp, :], OaT[:, 2 * hp, 0:n]), (PS[tb][:, 0:n], W_uvp[:, 2 * hp + 1, :], OaT[:, 2 * hp + 1, 0:n])], ["W_uvp", "OaT"])
                    A("dve", lambda g, hp=hp, tb=tb: g.tensor_tensor(catT[:, 2 + hp, 0:n], PS[tb][:, 0:n], sgb[:, hp, 0:n], ALU.mult), reads=[f"ps{tb}", "sgb"], writes=["catT_b"])
                    yield

            def gen_rwkv():
                A("act", lambda g: g.activation(th[:, 0:n], zs[0:64, 6, 0:n], AF.Tanh), reads=["zs"], writes=["th"])
                for c in range(2):
                    MM("ps0", [(PS[0][:, c * 128:c * 128 + n], dwa[0:64, c * 128:(c + 1) * 128], th[0:64, 0:n])], ["dwa", "th"])
                    A("act", lambda g, c=c: g.activation(lw[:, c, 0:n], PS[0][:, c * 128:c * 128 + n], AF.Sigmoid, bias=vec[:, 23 + c:24 + c]), reads=["ps0", "vec"], writes=["lw"])
                for c in range(2):
                    PE1(PS[1][:, c * 128:c * 128 + n], dwa[64:128, c * 128:(c + 1) * 128], zs[64:128, 6, 0:n], True, True, ["dwa", "zs"], ["ps1"], r32=False, rb=64)
                    A("act", lambda g, c=c: g.activation(av[:, c, 0:n], PS[1][:, c * 128:c * 128 + n], AF.Sigmoid, bias=vec[:, 25 + c:26 + c]), reads=["ps1", "vec"], writes=["av"])
                yield
                A("dve", lambda g: g.tensor_scalar(lw[:, :, 0:n], lw[:, :, 0:n], LOGW_C, None, ALU.mult), reads=["lw"], writes=["lw"])
                sm = scanm[:, 0:128] if nseg == 1 else scanm[:, 128:256]
                for c in range(2):
                    A("dve", lambda g, c=c, sm=sm: g.tensor_tensor_scan(G[:, c, 0:n], sm[:, 0:n], lw[:, c, 0:n], 0.0, ALU.mult, ALU.add), reads=["scanm", "lw"], writes=["G"])
                A("act", lambda g: g.activation(Epos[:, :, 0:n], G[:, :, 0:n], AF.Exp), reads=["G"], writes=["Epos"])
                A("act", lambda g: g.activation(Eneg[:, :, 0:n], G[:, :, 0:n], AF.Exp, scale=-1.0), reads=["G"], writes=["Eneg"])
                A("dve", lambda g: g.tensor_tensor(Eprev[:, :, 0:n], G[:, :, 0:n], lw[:, :, 0:n], ALU.subtract), reads=["G", "lw"], writes=["Eprev"])
                A("act", lambda g: g.activation(Eprev[:, :, 0:n], Eprev[:, :, 0:n], AF.Exp), reads=["Eprev"], writes=["Eprev"])
                yield
                A("dve", lambda g: g.tensor_tensor(kk[:, :, 0:n], zs[:, 2:4, 0:n], vec[:, 27:29].unsqueeze(2).broadcast_to([128, 2, n]), ALU.mult), reads=["zs", "vec"], writes=["kk"])
                A("dve", lambda g: g.tensor_tensor(bb[:, :, 0:n], kk[:, :, 0:n], kk[:, :, 0:n], ALU.mult), reads=["kk"], writes=["bb"])
                for c in range(2):
                    MM("ps0", [(PS[0][:, 256 + c * 128:256 + c * 128 + n], blk[:], bb[:, c, 0:n])], ["blk", "bb"])
                n2v = PS[0][:, 256:512].rearrange("p (c e) -> p c e", c=2)[:, :, 0:n]
                A("dve", lambda g: g.tensor_scalar(bb[:, :, 0:n], n2v, 1e-24, None, ALU.max), reads=["ps0"], writes=["bb"])
                A("act", lambda g: g.activation(bb[:, :, 0:n], bb[:, :, 0:n], AF.Ln), reads=["bb"], writes=["bb"])
                A("act", lambda g: g.activation(bb[:, :, 0:n], bb[:, :, 0:n], AF.Exp, scale=-0.5), reads=["bb"], writes=["bb"])
                A("dve", lambda g: g.tensor_tensor(kk[:, :, 0:n], kk[:, :, 0:n], bb[:, :, 0:n], ALU.mult), reads=["kk", "bb"], writes=["kk"])
                yield
                A("dve", lambda g: g.tensor_tensor(kmod[:, :, 0:n], av[:, :, 0:n], vec[:, 29:31].unsqueeze(2).broadcast_to([128, 2, n]), ALU.mult), reads=["av", "vec"], writes=["kmod"])
                A("dve", lambda g: g.tensor_tensor(kmod[:, :, 0:n], kmod[:, :, 0:n], omka[:, 0:2].unsqueeze(2).broadcast_to([128, 2, n]), ALU.add), reads=["kmod", "omka"], writes=["kmod"])
                A("dve", lambda g: g.tensor_tensor(kmod[:, :, 0:n], kmod[:, :, 0:n], zs[:, 2:4, 0:n], ALU.mult), reads=["kmod", "zs"], writes=["kmod"])
                A("dve", lambda g: g.tensor_tensor(bb[:, :, 0:n], kk[:, :, 0:n], av[:, :, 0:n], ALU.mult), reads=["kk", "av", "bb"], writes=["bb"])
                for c in range(2):
                    arv = AR[:, c, 0:2 * n].rearrange("p (s t e) -> p s t e", s=nseg, t=2)
                    A("dve", lambda g, c=c, arv=arv: g.scalar_tensor_tensor(arv[:, :, 0, :], segview(kk[:, c, 0:n]), -1.0, segview(Eprev[:, c, 0:n]), ALU.mult, ALU.mult), reads=["kk", "Eprev"], writes=["AR"])
                    A("dve", lambda g, c=c, arv=arv: g.tensor_tensor(arv[:, :, 1, :], segview(zs[:, c, 0:n]), segview(Epos[:, c, 0:n]), ALU.mult), reads=["zs", "Epos"], writes=["AR"])
                yield
                A("dve", lambda g: g.tensor_tensor(BT[:, :, 0:n], bb[:, :, 0:n], Eneg[:, :, 0:n], ALU.mult), reads=["bb", "Eneg"], writes=["BT"])
                A("dve", lambda g: g.tensor_tensor(KT[:, :, 0:n], kmod[:, :, 0:n], Eneg[:, :, 0:n], ALU.mult), reads=["kmod", "Eneg"], writes=["KT"])
                A("act", lambda g: g.copy(BTb[:, :, 0:n], BT[:, :, 0:n]), reads=["BT"], writes=["BTb"])
                A("act", lambda g: g.copy(KTb[:, :, 0:n], KT[:, :, 0:n]), reads=["KT"], writes=["KTb"])
                A("pool", lambda g: g.tensor_tensor(rkp[:, :, 0:n], zs[:, 0:2, 0:n], kmod[:, :, 0:n], ALU.mult), reads=["zs", "kmod", "G"], writes=["G"])
                for si, sg in enumerate(segs):
                    c0, C, slot_ = sg["col0"], sg["C"], sg["slot"]
                    nlev = {128: 6, 64: 5, 16: 3}[C]
                    mo = {128: 0, 64: 256, 16: 384}[C]
                    MM("ps0", [(PS[0][0:C, 0:4], rkp[:, c, c0:c0 + C], bonmat[:, c, :]) for c in range(2)], ["G", "bonmat"])
                    A("act", lambda g, C=C, si=si: g.copy(rkb[0:C, si, :], PS[0][0:C, 0:4]), reads=["ps0"], writes=["rkb"])
                    TR("ps1", [(PSB(1)[0:C, c * 128:(c + 1) * 128], BTb[:, c, c0:c0 + C]) for c in range(2)], ["BTb"], bf=True)
                    A("dve", lambda g, C=C, si=si: g.tensor_copy(Btok[0:C, si, :], PSB(1)[0:C, 0:256]), reads=["ps1"], writes=["Btok"])
                    yield
                    TR("ps0", [(PSB(0)[0:C, c * 128:(c + 1) * 128], KTb[:, c, c0:c0 + C]) for c in range(2)], ["KTb"], bf=True)
                    A("act", lambda g, C=C, si=si: g.copy(Ktok[0:C, si, :], PSB(0)[0:C, 0:256]), reads=["ps0"], writes=["Ktok"])
                    TR("ps1", [(PS[1][0:C, c * 128:(c + 1) * 128], zs[:, 4 + c, c0:c0 + C]) for c in range(2)], ["zs"])
                    A("dve", lambda g, C=C, si=si: g.tensor_copy(Vtok[0:C, si, :], PS[1][0:C, 0:256]), reads=["ps1"], writes=["Vtok"])
                    A("act", lambda g, C=C, si=si: g.copy(Vtokb[0:C, si, :], PS[1][0:C, 0:256]), reads=["ps1"], writes=["Vtokb"])
                    yield
                    for c in range(2):
                        def hrows(hh):
                            return slice(64 * hh, 64 * hh + 64)

                        for hh in range(2):
                            PE1(PS[hh][0:C, 0:2 * C], BTb[hrows(hh), c, c0:c0 + C], AR[hrows(hh), c, 2 * c0:2 * c0 + 2 * C], True, True, ["BTb", "AR"], [f"ps{hh}"], rb=64 * hh)
                        for hh in range(2):
                            PE1(PS[hh][0:C, 256:256 + 2 * C], KTb[hrows(hh), c, c0:c0 + C], AR[hrows(hh), c, 2 * c0:2 * c0 + 2 * C], True, True, ["KTb", "AR"], [f"ps{hh}"], rb=64 * hh)
                        mgb = mg[0:C, mo:mo + 2 * C].unsqueeze(1).broadcast_to([C, 2, 2 * C])
                        for hh in range(2):
                            A("dve", lambda g, C=C, mgb=mgb, hh=hh: g.tensor_tensor(Gm[0:C, hh, :].rearrange("p (t e) -> p t e", t=2)[:, :, 0:2 * C], PS[hh][0:C, :].rearrange("p (t e) -> p t e", t=2)[:, :, 0:2 * C], mgb, ALU.mult), reads=[f"ps{hh}", "mg"], writes=["Gm"])
                        yield
                        for hh in range(2):
                            PE1(PS[hh][0:C, 0:C], AR[hrows(hh), c, 2 * c0:2 * c0 + C], BTb[hrows(hh), c, c0:c0 + C], True, True, ["BTb", "AR"], [f"ps{hh}"], rb=64 * hh)
                        for hh in range(2):
                            A(("dve", "act")[hh] if False else "dve", lambda g, C=C, hh=hh: g.tensor_tensor(NXT[0:C, 0, hh, 0:C], PS[hh][0:C, 0:C], msl[0:C, 0:C], ALU.mult), reads=[f"ps{hh}", "msl"], writes=["NXT0"])
                        A("act", lambda g, C=C: g.copy(NX[0:C, 0, :, 0:C], Gm[0:C, :, 0:C]), reads=["Gm"], writes=["NX0"])
                        A("dve", lambda g, C=C: g.tensor_tensor(NW[0:C, 0, :, 0:C], Gm[0:C, :, 0:C], ident[0:C, 0:C].unsqueeze(1).broadcast_to([C, 2, C]), ALU.add), reads=["Gm", "ident"], writes=["NW0"])
                        for lev in range(nlev):
                            yield
                            a, b_ = lev % 2, (lev + 1) % 2
                            lastlev = (lev == nlev - 1)

                            def fnN(g, ps, o, L, Rr, C=C):
                                ins = None
                                for hh in range(2):
                                    ins = g.matmul(ps[0:C, o + hh * 128:o + hh * 128 + C], lhsT=L[:, hh, :], rhs=Rr[:, hh, :], start=True, stop=True)
                                return ins
                            A("pe", lambda g, fnN=fnN, a=a, C=C: fnN(g, PS[0], 0, NX[0:C, a, :, 0:C], NXT[0:C, a, :, 0:C]), reads=[f"NX{a}", f"NXT{a}"], writes=["ps0"])
                            A("act", lambda g, C=C, b_=b_: g.copy(NXT[0:C, b_, :, 0:C], PS[0][0:C, 0:256].rearrange("p (h e) -> p h e", h=2)[:, :, 0:C]), reads=["ps0"], writes=[f"NXT{b_}"])
                            if not lastlev:
                                A("pe", lambda g, fnN=fnN, a=a, C=C: fnN(g, PS[1], 0, NXT[0:C, a, :, 0:C], NX[0:C, a, :, 0:C]), reads=[f"NX{a}", f"NXT{a}"], writes=["ps1"])
                                A("dve", lambda g, C=C, b_=b_: g.tensor_copy(NX[0:C, b_, :, 0:C], PS[1][0:C, 0:256].rearrange("p (h e) -> p h e", h=2)[:, :, 0:C]), reads=["ps1"], writes=[f"NX{b_}"])
                            A("pe", lambda g, fnN=fnN, a=a, b_=b_, C=C: fnN(g, PS[0], 256, NXT[0:C, b_, :, 0:C], NW[0:C, a, :, 0:C]), reads=[f"NXT{b_}", f"NW{a}"], writes=["ps0"])
                            A("dve", lambda g, C=C, a=a, b_=b_, lastlev=lastlev: g.tensor_tensor(NW[0:C, b_, :, 0:C], PS[0][0:C, 256:512].rearrange("p (h e) -> p h e", h=2)[:, :, 0:C], NW[0:C, a, :, 0:C], ALU.add), reads=["ps0", f"NW{a}"], writes=[f"NW{b_}"])
                        yield
                        wf = nlev % 2
                        Hc = Hp[:, slot_, c, :]
                        hk = f"Hp{slot_}_{c}"

                        A("act", lambda g, Hc=Hc: g.copy(Hb[:], Hc), reads=[hk], writes=["Hb"])
                        for hh in range(2):
                            PE1(PS[hh][0:C, 256:320], AR[hrows(hh), c, 2 * c0:2 * c0 + C], Hb[hrows(hh), hh * 64:hh * 64 + 64], True, False, ["AR", "Hb"], [f"ps{hh}"], rb=64 * hh)
                        for hh in range(2):
                            h = 2 * c + hh
                            PE1(PS[hh][0:C, 256:320], Gm[0:C, hh, 256:256 + C], Vtokb[0:C, si, h * 64:h * 64 + 64], False, True, ["Gm", "Vtokb"], [f"ps{hh}"], rb=0)
                        A("act", lambda g, C=C: g.copy(X1[0:C, 0:64], PS[0][0:C, 256:320]), reads=["ps0"], writes=["X1"])
                        A("dve", lambda g, C=C: g.tensor_copy(X1[0:C, 64:128], PS[1][0:C, 256:320]), reads=["ps1"], writes=["X1"])

                        def fnU(g, C=C, wf=wf):
                            ins = None
                            for hh in range(2):
                                ins = g.matmul(PS[1][0:C, 384 + hh * 64:384 + hh * 64 + 64], lhsT=NW[0:C, wf, hh, 0:C], rhs=X1[0:C, hh * 64:hh * 64 + 64], start=True, stop=True)
                            return ins
                        A("pe", fnU, reads=[f"NW{wf}", "X1"], writes=["ps1"])
                        A("dve", lambda g, C=C: g.tensor_copy(U[0:C, :], PS[1][0:C, 384:512]), reads=["ps1"], writes=["U"])

                        for hh in (1, 0):
                            PE1(PS[hh][0:C, 320:384], AR[hrows(hh), c, 2 * c0 + C:2 * c0 + 2 * C], Hb[hrows(hh), hh * 64:hh * 64 + 64], True, False, ["AR", "Hb"], [f"ps{hh}"], rb=64 * hh)
                        for hh in (0, 1):
                            h = 2 * c + hh
                            PE1(PS[hh][0:C, 320:384], Gm[0:C, hh, C:2 * C], U[0:C, hh * 64:hh * 64 + 64], False, False, ["Gm", "U"], [f"ps{hh}"], rb=0)
                            PE1(PS[hh][0:C, 320:384], Gm[0:C, hh, 256 + C:256 + 2 * C], Vtokb[0:C, si, h * 64:h * 64 + 64], False, True, ["Gm", "Vtokb"], [f"ps{hh}"], rb=0)
                        A("act", lambda g, C=C, c=c: g.copy(yraw[0:C, c * 128:c * 128 + 64], PS[0][0:C, 320:384]), reads=["ps0"], writes=["yraw"])
                        A("dve", lambda g, C=C, c=c: g.tensor_copy(yraw[0:C, c * 128 + 64:c * 128 + 128], PS[1][0:C, 320:384]), reads=["ps1"], writes=["yraw"])
                        MM("ps0", [(PS[0][:, 0:128], R(identr[:]), R(Hc)), (PS[0][:, 0:128], Btok[0:C, si, c * 128:(c + 1) * 128], U[0:C, :]),
                                   (PS[0][:, 0:128], Ktok[0:C, si, c * 128:(c + 1) * 128], Vtokb[0:C, si, c * 128:(c + 1) * 128])], ["identr", hk, "Btok", "Ktok", "Vtokb", "U"])
                        A("dve", lambda g, Hc=Hc, c=c, c0=c0, C=C: g.scalar_tensor_tensor(R(Hc), PS[0][:, 0:128], Epos[:, c, c0 + C - 1:c0 + C], blk[:], ALU.mult, ALU.mult), reads=["ps0", "Epos", "blk"], writes=[hk])
                        if sg["last"]:
                            TR("ps1", [(PS[1][:, 0:128], Hc)], [hk])
                            A("act", lambda g: g.copy(wko[:], PS[1][:, 0:128]), reads=["ps1"], writes=["wko"])
                            for hh in range(2):
                                h = 2 * c + hh
                                DMA(sg["wkv_out"][h * 64:(h + 1) * 64, :], wko[hrows(hh), hh * 64:hh * 64 + 64], reads=["wko"])
                    yield
                    Yv = yraw[0:C, :].rearrange("p (h e) -> p h e", h=4)
                    ykeys = ["yraw"]
                    A("dve", lambda g, C=C, Yv=Yv: g.tensor_reduce(gst[0:C, 0:4], Yv, AX.X, ALU.add), reads=ykeys, writes=["gst"])
                    A("dve", lambda g, C=C: g.tensor_scalar(gst[0:C, 0:4], gst[0:C, 0:4], 1.0 / 64, None, ALU.mult), reads=["gst"], writes=["gst"])
                    ycv = yc[0:C, :].rearrange("p (h e) -> p h e", h=4)
                    A("dve", lambda g, C=C, Yv=Yv, ycv=ycv: g.tensor_tensor(ycv, Yv, gst[0:C, 0:4].unsqueeze(2).broadcast_to([C, 4, 64]), ALU.subtract), reads=ykeys + ["gst"], writes=["yc"])
                    A("dve", lambda g, C=C: g.tensor_tensor(ysq[0:C, :], yc[0:C, :], yc[0:C, :], ALU.mult), reads=["yc"], writes=["ybv"])
                    A("dve", lambda g, C=C: g.tensor_reduce(gst[0:C, 4:8], ysq[0:C, :].rearrange("p (h e) -> p h e", h=4), AX.X, ALU.add), reads=["ybv"], writes=["gst"])
                    A("act", lambda g, C=C: g.activation(gst[0:C, 4:8], gst[0:C, 4:8], AF.Ln, bias=epsb[0:C, 1:2], scale=1.0 / 64), reads=["gst", "epsb"], writes=["gst"])
                    A("act", lambda g, C=C: g.activation(gst[0:C, 4:8], gst[0:C, 4:8], AF.Exp, scale=-0.5), reads=["gst"], writes=["gst"])
                    A("dve", lambda g, C=C, ycv=ycv: g.tensor_tensor(ycv, ycv, gst[0:C, 4:8].unsqueeze(2).broadcast_to([C, 4, 64]), ALU.mult), reads=["yc", "gst"], writes=["yc"])
                    A("dve", lambda g, C=C: g.tensor_tensor(yc[0:C, :], yc[0:C, :], lnw_bc[0:C, :], ALU.mult), reads=["yc", "lnw_bc"], writes=["yc"])
                    A("dve", lambda g, C=C: g.tensor_tensor(yc[0:C, :], yc[0:C, :], lnb_bc[0:C, :], ALU.add), reads=["yc", "lnb_bc"], writes=["yc"])
                    A("dve", lambda g, C=C, si=si: g.tensor_tensor(ybv[0:C, :].rearrange("p (h e) -> p h e", h=4), Vtok[0:C, si, :].rearrange("p (h e) -> p h e", h=4), rkb[0:C, si, :].unsqueeze(2).broadcast_to([C, 4, 64]), ALU.mult), reads=["Vtok", "rkb", "gst"], writes=["ybv"])
                    A("dve", lambda g, C=C: g.tensor_tensor(yc[0:C, :], yc[0:C, :], ybv[0:C, :], ALU.add), reads=["yc", "ybv"], writes=["yc"])
                    A("dve", lambda g, C=C, si=si: g.tensor_tensor(ycb[0:C, :], yc[0:C, :], sgc[0:C, si, :], ALU.mult), reads=["yc", "sgc"], writes=["ycb"])
                    TR("ps1", [(PSB(1)[:, cc * 128:cc * 128 + C], ycb[0:C, cc * 128:(cc + 1) * 128]) for cc in range(2)], ["ycb"], bf=True)
                    A("dve", lambda g, C=C, c0=c0: g.tensor_copy(catT[:, 6:8, c0:c0 + C], PSB(1)[:, 0:256].rearrange("p (j e) -> p j e", j=2)[:, :, 0:C]), reads=["ps1"], writes=["catT_c"])

            gens = [gen_attn(), gen_rwkv()]
            weights = [2, 1]
            while gens:
                done = []
                for g_, w_ in zip(gens, weights):
                    for _ in range(w_):
                        try:
                            next(g_)
                        except StopIteration:
                            done.append(g_)
                            break
                for g_ in done:
                    ix = gens.index(g_)
                    gens.pop(ix)
                    weights.pop(ix)
            if t.get('next') is not None:
                front1b(t['next'])
                t['next']['front_done'] = True
            for half in range(2):
                MM(f"ps{2 + half}", [(PS[2 + half][0:n, :], catT[:, f, 0:n], W_out[:, f, half * 512:(half + 1) * 512]) for f in range(8)], ["catT_a", "catT_b", "catT_c", "W_out"])
                A("dve", lambda g, half=half: g.tensor_tensor(X[0:n, half * 512:(half + 1) * 512], X[0:n, half * 512:(half + 1) * 512], PS[2 + half][0:n, :], ALU.add), reads=[f"ps{2 + half}", xk], writes=[xk])
            if x_dst is not None:
                DMA(x_dst, X[0:n, :], reads=[xk], writes=["scr"])
            if y_dst is not None:
                A("dve", lambda g: g.memset(ss[:, 3:4], 0.0), writes=["ss"])
                A("act", lambda g: g.activation(hf[0:n, :], X[0:n, :], AF.Square, accum_out=ss[0:n, 3:4]), reads=[xk, "ss"], writes=[hfk, "ss"])
                rstd_from_ss(n, 3, D, RMS_EPS)
                A("dve", lambda g: g.scalar_tensor_tensor(hf[0:n, :], X[0:n, :], rs_[0:n, 3:4], fg_bc[0:n, :], ALU.mult, ALU.mult), reads=[xk, "rs_", "fg_bc"], writes=[hfk])
                DMA(y_dst, hf[0:n, :], reads=[hfk])

        tiles = []
        for l in range(2):
            for ti in range(NPT):
                tok0 = 0 if ti == 0 else 16 + (ti - 1) * 128
                n = 16 if ti == 0 else 128
                kts = []
                if ti >= 1:
                    kts.append((CKVT_p[:, 0:16], KRT_p[0:32, 0:16], CKV_p[0:16, 0, :], 16, False, "kvp"))
                for kj in range(1, ti):
                    k0 = 16 + (kj - 1) * 128
                    kts.append((CKVT_p[:, k0:k0 + 128], KRT_p[0:32, k0:k0 + 128], CKV_p[:, kj, :], 128, False, "kvp"))
                kts.append((CKVT_p[:, tok0:tok0 + n], KRT_p[0:32, tok0:tok0 + n], CKV_p[0:n, ti, :], n, True, "kvp"))
                sg = dict(slot=0, col0=0, C=n, pos0=tok0, ckv_out=ckv_p[l, tok0:tok0 + n, :], kr_out=kr_p[l, tok0:tok0 + n, :],
                          ktiles=kts, last=(ti == NPT - 1), CKVTown=CKVT_p[:, tok0:tok0 + n], KRTown=KRT_p[0:32, tok0:tok0 + n],
                          CKVown=CKV_p[0:n, ti, 0:128], ownkey="kvp", conv_out=conv_p[l], shift_out=shift_p[l], wkv_out=wkv_p[l], preattn=None)
                tiles.append(dict(l=l, n=n, segs=[sg], first=(ti == 0), kind="p",
                                  x_src=(xp[tok0:tok0 + n, :] if l == 0 else scr_p[tok0:tok0 + n, :]),
                                  x_dst=(scr_p[tok0:tok0 + n, :] if l == 0 else None),
                                  y_dst=(y_p[tok0 - 16:tok0 - 16 + n, :] if (l == 1 and ti >= 1) else None)))
            for tj in range(NS // 2):
                segs = []
                for si in range(2):
                    b = 2 * tj + si
                    slot_ = 1 + si

                    def preattn(l=l, b=b):
                        for kt in range(PAST // 128):
                            bs = stage_load(cckv[l, b, kt * 128:(kt + 1) * 128, :], 128)
                            DMA(stage[:, bs, 128:160], ckr[l, b, kt * 128:(kt + 1) * 128, :], writes=[f"stage{bs}"])
                            A("pool", lambda g, bs=bs, kt=kt: g.tensor_copy(CKV_s[:, kt, 0:128], stage[:, bs, 0:128]), reads=[f"stage{bs}"], writes=["kvs"])
                            TR("ps0", [(PS[0][:, 0:128], stage[:, bs, 0:128]), (PS[0][0:32, 128:256], stage[:, bs, 128:160])], [f"stage{bs}"])
                            A("act", lambda g, kt=kt: g.copy(CKVT_s[:, kt * 128:(kt + 1) * 128], PS[0][:, 0:128]), reads=["ps0"], writes=["kvs"])
                            A("dve", lambda g, kt=kt: g.tensor_copy(KRT_s[0:32, kt * 128:(kt + 1) * 128], PS[0][0:32, 128:256]), reads=["ps0"], writes=["kvs"])

                    def prestate(l=l, b=b, slot_=slot_):
                        for cc in range(2):
                            DMA(uxcar[:, slot_, cc, :], sconvT[l, b, cc * 128:(cc + 1) * 128, :], writes=["uxcar"])
                        DMA(zcar[:, slot_, :], sshiftT[l, b], writes=["zcar"])
                        for c in range(2):
                            A("pool", lambda g: g.memset(Sp[:], 0.0), writes=["Sp"])
                            for hh in range(2):
                                h = 2 * c + hh
                                DMA(Sp[64 * hh:64 * hh + 64, 64 * hh:64 * hh + 64], swkv[l, b, h * 64:(h + 1) * 64, :], writes=["Sp"])
                            TR("ps0", [(PS[0][:, 0:128], Sp[:])], ["Sp"])
                            A("dve", lambda g, slot_=slot_, c=c: g.tensor_copy(R(Hp[:, slot_, c, :]), PS[0][:, 0:128]), reads=["ps0"], writes=[f"Hp{slot_}_{c}"])
                    kts = []
                    for kt in range(PAST // 128):
                        kts.append((CKVT_s[:, kt * 128:(kt + 1) * 128], KRT_s[0:32, kt * 128:(kt + 1) * 128], CKV_s[:, kt, :], 128, False, "kvs"))
                    kts.append((CKVT_o[:, si, :], KRT_o[0:32, si, :], CKV_o[0:64, si, :], 64, True, f"kvo{si}"))
                    segs.append(dict(slot=slot_, col0=64 * si, C=64, pos0=TP, ckv_out=ckv_s[l, b], kr_out=kr_s[l, b], ktiles=kts, last=True,
                                     CKVTown=CKVT_o[:, si, :], KRTown=KRT_o[0:32, si, :], CKVown=CKV_o[0:64, si, 0:128], ownkey=f"kvo{si}",
                                     conv_out=conv_s[l, b], shift_out=shift_s[l, b], wkv_out=wkv_s[l, b], preattn=preattn, prestate=prestate))
                r0 = tj * 128
                tiles.append(dict(l=l, n=128, segs=segs, first=False, kind="s",
                                  x_src=(xs[r0:r0 + 128, :] if l == 0 else scr_s[r0:r0 + 128, :]),
                                  x_dst=(scr_s[r0:r0 + 128, :] if l == 0 else None),
                                  y_dst=(y_s[r0:r0 + 128, :] if l == 1 else None)))
        for i, t in enumerate(tiles):
            t["par"] = i % 2
            t["next"] = tiles[i + 1] if (i + 1 < len(tiles) and tiles[i + 1]["l"] == t["l"]) else None
        A("pool", lambda g: g.memset(epsb[:, 0:1], RMS_EPS), writes=["epsb"])
        A("pool", lambda g: g.memset(epsb[:, 1:2], GN_EPS), writes=["epsb"])
        cur_l = -1
        for i, t in enumerate(tiles):
            if t["l"] != cur_l:
                cur_l = t["l"]
                load_weights(cur_l)
                load_x(t)
                load_rope(t)
            if t["kind"] == "p" and t["first"]:
                A("pool", lambda g: g.memset(uxcar[:, 0, :, :], 0.0), writes=["uxcar"])
                A("pool", lambda g: g.memset(zcar[:, 0, :], 0.0), writes=["zcar"])
                for c in range(2):
                    A("dve", lambda g, c=c: g.tensor_scalar(R(Hp[:, 0, c, :]), blk[:], 0.0, None, ALU.mult), reads=["blk"], writes=[f"Hp0_{c}"])
            if t["kind"] == "s":
                for sg in t["segs"]:
                    sg["prestate"]()
            if i + 1 < len(tiles) and tiles[i + 1]["l"] == t["l"]:
                load_x(tiles[i + 1])
                load_rope(tiles[i + 1])
            do_tile(t)
        S.emit()
    return nc


def host_consts(TP, PAST):
    ident = np.eye(128, dtype=np.float32)
    blk = np.zeros((128, 128), np.float32)
    blk[:64, :64] = 1
    blk[64:, 64:] = 1
    mg = np.zeros((128, 416), np.float32)
    for C, mo in ((128, 0), (64, 256), (16, 384)):
        s = np.arange(C)[:, None]
        t = np.arange(C)[None, :]
        mg[:C, mo:mo + C] = (s < t)
        mg[:C, mo + C:mo + 2 * C] = (s <= t)
    tt = np.arange(128)[:, None]
    s_ = np.arange(128)[None, :]
    sl = (s_ < tt).astype(np.float32)
    k = np.arange(128)[:, None]
    q = np.arange(128)[None, :]
    am = np.ones((128, 128), np.float32)
    am[(k >= 64) & (q < 64)] = 0
    scan = np.ones((128, 256), np.float32)
    scan[:, 0] = 0
    scan[:, 128] = 0
    scan[:, 128 + 64] = 0
    pos = np.concatenate([np.arange(TP), PAST + np.arange(64)]).astype(np.float32)
    inv = (10000.0 ** (-np.arange(16, dtype=np.float32) * 2.0 / 32)).astype(np.float32)
    ang = (pos[:, None] * inv[None, :]).astype(np.float32)
    cos = np.cos(ang).astype(np.float32)
    sin = np.sin(ang).astype(np.float32)
    rope_tm = np.concatenate([cos, sin], axis=1).astype(np.float32)
    rope_fm = np.stack([np.concatenate([cos, cos], 1).T, np.concatenate([sin, sin], 1).T], axis=1)
    return dict(c_ident=ident, c_blk=blk, c_mg=mg, c_sl=sl, c_am=am, c_scan=scan,
                rope_tm=np.ascontiguousarray(rope_tm), rope_fm=np.ascontiguousarray(rope_fm.astype(np.float32)))


def col128(v):
    v = np.asarray(v, np.float32)
    return np.ascontiguousarray(v.reshape(-1, 128).T)


def run(inputs, n_cores=8):
    f = lambda k: np.asarray(inputs[k], np.float32)
    x_prompt, x_sample = f("x_prompt"), f("x_sample")
    BATCH, SEQ, _ = x_prompt.shape
    DEC_BATCH = x_sample.shape[0]
    PAST = inputs["cache_ckv"].shape[2]
    TP = SEQ + 16
    NS = DEC_BATCH // n_cores
    if NS % 2:
        raise ValueError("NS must be even")
    nc = build_nc(TP, NS, PAST)
    consts = host_consts(TP, PAST)
    meta = f("meta_tokens")
    vecs = np.zeros((2, 128, NV), np.float32)
    for l in range(2):
        vecs[l, :, 0:8] = col128(f("norm_g")[l])
        vecs[l, :, 8:10] = col128(f("q_norm_g")[l])
        cw = f("conv_w")[l]
        for cc in range(2):
            for j in range(3):
                vecs[l, :, 10 + cc * 3 + j] = cw[j, cc * 128:(cc + 1) * 128]
        vecs[l, :, 16:23] = col128(f("shift_mu")[l])
        vecs[l, :, 23:25] = col128(f("decay_w0")[l])
        vecs[l, :, 25:27] = col128(f("iclr_a0")[l])
        vecs[l, :, 27:29] = col128(f("key_kk")[l])
        vecs[l, :, 29:31] = col128(f("key_ka")[l])
        vecs[l, :, 31:33] = col128(f("bonus_rk")[l])
    shared = dict(w_in=f("w_in"), w_out=f("w_out"), w_uq=f("w_uq"), w_ukv=f("w_ukv"), dw2=f("decay_w2"), a2=f("iclr_a2"),
                  vecs=vecs, kvg=f("kv_norm_g")[:, None, :].copy(), lnw=f("lnx_w")[:, None, :].copy(), lnb=f("lnx_b")[:, None, :].copy(),
                  fg=f("final_g")[None, :].copy(), **consts)
    in_maps = []
    for c in range(n_cores):
        pb = c % BATCH
        sl = slice(c * NS, (c + 1) * NS)
        m = dict(shared)
        m["xp"] = np.ascontiguousarray(np.concatenate([meta, x_prompt[pb]], axis=0))
        m["xs"] = np.ascontiguousarray(x_sample[sl].reshape(NS * 64, D))
        m["cckv"] = np.ascontiguousarray(f("cache_ckv")[:, sl])
        m["ckr"] = np.ascontiguousarray(f("cache_krope")[:, sl])
        m["sconvT"] = np.ascontiguousarray(f("state_conv")[:, sl].transpose(0, 1, 3, 2))
        m["sshiftT"] = np.ascontiguousarray(f("state_shift")[:, sl].reshape(2, NS, 7, 128).transpose(0, 1, 3, 2))
        m["swkv"] = np.ascontiguousarray(f("state_wkv")[:, sl].reshape(2, NS, 256, 64))
        in_maps.append(m)
    res = run_bass_kernel_spmd(nc, in_maps, core_ids=list(range(n_cores)))
    R = res.results
    y_prompt = np.stack([R[b]["y_p"] for b in range(BATCH)])
    y_sample = np.concatenate([R[c]["y_s"].reshape(NS, 64, D) for c in range(n_cores)], axis=0)
    ckv_p = np.stack([R[b]["ckv_p"] for b in range(BATCH)], axis=1)
    kr_p = np.stack([R[b]["kr_p"] for b in range(BATCH)], axis=1)
    conv_p = np.stack([R[b]["conv_p"].transpose(0, 2, 1) for b in range(BATCH)], axis=1)
    shift_p = np.stack([R[b]["shift_p"].transpose(0, 2, 1).reshape(2, 896) for b in range(BATCH)], axis=1)
    wkv_p = np.stack([R[b]["wkv_p"].reshape(2, 4, 64, 64) for b in range(BATCH)], axis=1)
    ckv_s = np.concatenate([R[c]["ckv_s"] for c in range(n_cores)], axis=1)
    kr_s = np.concatenate([R[c]["kr_s"] for c in range(n_cores)], axis=1)
    conv_s = np.concatenate([R[c]["conv_s"].transpose(0, 1, 3, 2) for c in range(n_cores)], axis=1)
    shift_s = np.concatenate([R[c]["shift_s"].transpose(0, 1, 3, 2).reshape(2, NS, 896) for c in range(n_cores)], axis=1)
    wkv_s = np.concatenate([R[c]["wkv_s"].reshape(2, NS, 4, 64, 64) for c in range(n_cores)], axis=1)
    outs = (y_prompt, y_sample, ckv_p, kr_p, conv_p, shift_p, wkv_p, ckv_s, kr_s, conv_s, shift_s, wkv_s)
    return tuple(np.ascontiguousarray(o, dtype=np.float32) for o in outs)


def kernel(**inputs):
    return run(inputs, n_cores=8)
```
